# Optimizing a Trainium2 kernel written in Bass

```python
import jax, jax.numpy as jnp
from jax import lax
import numpy as np

D_MODEL = 2048
BATCH = 8
SEQ = 2048
DEPTH = 1

N_META = 16
D_RNN = D_MODEL
N_RNN_BLOCKS = 8
RNN_BLOCK = D_RNN // N_RNN_BLOCKS
CONV_WIDTH = 4
LRU_C = 8.0
LRU_MIN_RAD = 0.9
LRU_MAX_RAD = 0.999
HEAD_DIM = 64
N_Q_HEADS = D_MODEL // HEAD_DIM
N_KV_HEADS = N_Q_HEADS // 8
GROUP = N_Q_HEADS // N_KV_HEADS
D_ATTN = N_Q_HEADS * HEAD_DIM
D_KV = N_KV_HEADS * HEAD_DIM
WINDOW = 128
BLOCK = 128
ROPE_THETA = 10000.0
NEG_INF = -1e30
N_BRANCHES = 2
LN_EPS = 1e-5
DEEPNORM_ALPHA = (2.0 * DEPTH) ** 0.25
DEEPNORM_BETA = (8.0 * DEPTH) ** -0.25
OFF_GR = D_RNN
OFF_Q = 2 * D_RNN
OFF_K = OFF_Q + D_ATTN
OFF_V = OFF_K + D_KV
OFF_GA = OFF_V + D_KV
OFF_G = OFF_GA + D_ATTN
D_IN = OFF_G + N_BRANCHES * D_MODEL

kernel_name = "hybrid_rglru_swa_sink_gated_merge"


def layer_norm(x, g, b):
    xf = x.astype(jnp.float32)
    mu = xf.mean(-1, keepdims=True)
    var = jnp.square(xf - mu).mean(-1, keepdims=True)
    y = (xf - mu) * lax.rsqrt(var + LN_EPS)
    return (y * g.astype(jnp.float32) + b.astype(jnp.float32)).astype(x.dtype)


def rope(x, pos):
    half = HEAD_DIM // 2
    inv = ROPE_THETA ** (-jnp.arange(half, dtype=jnp.float32) / half)
    ang = pos.astype(jnp.float32)[:, None] * inv[None, :]
    cos = jnp.cos(ang)[None, :, None, :]
    sin = jnp.sin(ang)[None, :, None, :]
    xf = x.astype(jnp.float32)
    x1, x2 = xf[..., :half], xf[..., half:]
    return jnp.concatenate([x1 * cos - x2 * sin, x2 * cos + x1 * sin], axis=-1).astype(x.dtype)


def causal_depthwise_conv(x, w, b):
    T = x.shape[1]
    xp = jnp.pad(x, ((0, 0), (CONV_WIDTH - 1, 0), (0, 0)))
    y = b
    for k in range(CONV_WIDTH):
        s = CONV_WIDTH - 1 - k
        y = y + w[k] * xp[:, s:s + T]
    return y


def rg_lru(x, w_ra, b_ra, w_ri, b_ri, lam):
    B, T, _ = x.shape
    xb = x.reshape(B, T, N_RNN_BLOCKS, RNN_BLOCK)
    gate_r = jax.nn.sigmoid((jnp.einsum('btnc,ncd->btnd', xb, w_ra).reshape(B, T, D_RNN) + b_ra).astype(jnp.float32))
    gate_i = jax.nn.sigmoid((jnp.einsum('btnc,ncd->btnd', xb, w_ri).reshape(B, T, D_RNN) + b_ri).astype(jnp.float32))
    log_a = LRU_C * gate_r * jax.nn.log_sigmoid(lam.astype(jnp.float32))
    a = jnp.exp(log_a)
    mult = jnp.sqrt(-jnp.expm1(2.0 * log_a))
    mult = jnp.where((jnp.arange(T) == 0)[None, :, None], 1.0, mult)
    u = mult * gate_i * x.astype(jnp.float32)

    def combine(left, right):
        a1, b1 = left
        a2, b2 = right
        return a1 * a2, a2 * b1 + b2

    _, h = lax.associative_scan(combine, (a, u), axis=1)
    return h.astype(x.dtype)


def sliding_window_sink_attention(q, k, v, sinks):
    B, T = q.shape[:2]
    pad = BLOCK - N_META
    Lp = T + pad
    NB = Lp // BLOCK
    padf = lambda t: jnp.pad(t, ((0, 0), (pad, 0), (0, 0), (0, 0)))
    qb = padf(q).reshape(B, NB, BLOCK, N_KV_HEADS, GROUP, HEAD_DIM)
    kb = padf(k).reshape(B, NB, BLOCK, N_KV_HEADS, HEAD_DIM)
    vb = padf(v).reshape(B, NB, BLOCK, N_KV_HEADS, HEAD_DIM)

    def banded(t):
        prev = jnp.pad(t, ((0, 0), (1, 0), (0, 0), (0, 0), (0, 0)))[:, :NB]
        meta = jnp.broadcast_to(t[:, :1, pad:], (B, NB, N_META, N_KV_HEADS, HEAD_DIM))
        return jnp.concatenate([meta, prev, t], axis=2)

    kk, vv = banded(kb), banded(vb)

    qi = jnp.arange(NB)[:, None] * BLOCK + jnp.arange(BLOCK)[None, :]
    jb = (jnp.arange(NB)[:, None] - 1) * BLOCK + jnp.arange(2 * BLOCK)[None, :]
    jm = pad + jnp.arange(N_META)
    band_ok = ((jb[:, None, :] >= BLOCK) & (jb[:, None, :] <= qi[:, :, None])
               & (qi[:, :, None] - jb[:, None, :] < WINDOW))
    meta_ok = jnp.broadcast_to(jm[None, None, :] <= qi[:, :, None], (NB, BLOCK, N_META))
    mask = jnp.concatenate([meta_ok, band_ok], axis=-1)

    s = jnp.einsum('bnqkgd,bnskd->bnkgqs', qb, kk).astype(jnp.float32) * (HEAD_DIM ** -0.5)
    s = jnp.where(mask[None, :, None, None], s, NEG_INF)
    sink = sinks.astype(jnp.float32).reshape(N_KV_HEADS, GROUP)[None, None, :, :, None, None]
    m = jnp.maximum(s.max(-1, keepdims=True), sink)
    p = jnp.exp(s - m)
    denom = p.sum(-1, keepdims=True) + jnp.exp(sink - m)
    o = jnp.einsum('bnkgqs,bnskd->bnqkgd', (p / denom).astype(v.dtype), vv)
    return o.reshape(B, Lp, D_ATTN)[:, pad:]


def hybrid_layer(h, pos, w_in, b_in, conv_w, conv_b, w_ra, b_ra, w_ri, b_ri, lam, sinks,
                 w_rnn_out, w_attn_out, w_o, b_o, ln_g, ln_b):
    B, T, _ = h.shape
    z = h @ w_in + b_in
    xr, gr, q, k, v, ga, mg = jnp.split(z, [OFF_GR, OFF_Q, OFF_K, OFF_V, OFF_GA, OFF_G], axis=-1)

    hr = rg_lru(causal_depthwise_conv(xr, conv_w, conv_b), w_ra, b_ra, w_ri, b_ri, lam)
    y_a = (hr * jax.nn.silu(gr)) @ w_rnn_out

    q = rope(q.reshape(B, T, N_Q_HEADS, HEAD_DIM), pos)
    k = rope(k.reshape(B, T, N_KV_HEADS, HEAD_DIM), pos)
    v = v.reshape(B, T, N_KV_HEADS, HEAD_DIM)
    o = sliding_window_sink_attention(q, k, v, sinks)
    y_b = (o * jax.nn.silu(ga)) @ w_attn_out

    g = jax.nn.sigmoid(mg.astype(jnp.float32)).astype(h.dtype)
    mixed = g[..., :D_MODEL] * y_a + g[..., D_MODEL:] * y_b
    out = mixed @ w_o + b_o
    return layer_norm(DEEPNORM_ALPHA * h + out, ln_g, ln_b)


def setup_inputs(seed: int = 0) -> dict:
    key = jax.random.key(seed)
    ks = jax.random.split(key, 24)
    f32 = jnp.float32
    nrm = lambda k, shape, scale: jax.random.normal(k, shape, f32) * scale
    u = jax.random.uniform(ks[12], (DEPTH, D_RNN), f32, LRU_MIN_RAD, LRU_MAX_RAD)
    s_rad = u ** (1.0 / LRU_C)
    lru_lambda = jnp.log(s_rad) - jnp.log1p(-s_rad)
    return {
        "x": jax.random.normal(ks[0], (BATCH, SEQ, D_MODEL), f32),
        "meta_tokens": nrm(ks[1], (N_META, D_MODEL), 1.0),
        "ln_emb_g": 1.0 + nrm(ks[2], (D_MODEL,), 0.01),
        "ln_emb_b": nrm(ks[3], (D_MODEL,), 0.01),
        "w_in": nrm(ks[4], (DEPTH, D_MODEL, D_IN), D_MODEL ** -0.5),
        "b_in": nrm(ks[5], (DEPTH, D_IN), 0.01),
        "conv_w": nrm(ks[6], (DEPTH, CONV_WIDTH, D_RNN), CONV_WIDTH ** -0.5),
        "conv_b": nrm(ks[7], (DEPTH, D_RNN), 0.01),
        "w_ra": nrm(ks[8], (DEPTH, N_RNN_BLOCKS, RNN_BLOCK, RNN_BLOCK), RNN_BLOCK ** -0.5),
        "b_ra": nrm(ks[9], (DEPTH, D_RNN), 0.01),
        "w_ri": nrm(ks[10], (DEPTH, N_RNN_BLOCKS, RNN_BLOCK, RNN_BLOCK), RNN_BLOCK ** -0.5),
        "b_ri": nrm(ks[11], (DEPTH, D_RNN), 0.01),
        "lru_lambda": lru_lambda,
        "sinks": nrm(ks[13], (DEPTH, N_Q_HEADS), 0.5),
        "w_rnn_out": nrm(ks[14], (DEPTH, D_RNN, D_MODEL), D_RNN ** -0.5 * DEEPNORM_BETA),
        "w_attn_out": nrm(ks[15], (DEPTH, D_ATTN, D_MODEL), D_ATTN ** -0.5 * DEEPNORM_BETA),
        "w_o": nrm(ks[16], (DEPTH, D_MODEL, D_MODEL), D_MODEL ** -0.5 * DEEPNORM_BETA),
        "b_o": nrm(ks[17], (DEPTH, D_MODEL), 0.01),
        "ln_g": 1.0 + nrm(ks[18], (DEPTH, D_MODEL), 0.01),
        "ln_b": nrm(ks[19], (DEPTH, D_MODEL), 0.01),
    }


def reference(x, meta_tokens, ln_emb_g, ln_emb_b, w_in, b_in, conv_w, conv_b, w_ra, b_ra, w_ri, b_ri,
              lru_lambda, sinks, w_rnn_out, w_attn_out, w_o, b_o, ln_g, ln_b):
    B = x.shape[0]
    meta = jnp.broadcast_to(meta_tokens.astype(x.dtype)[None], (B, N_META, D_MODEL))
    h = jnp.concatenate([meta, x], axis=1)
    h = layer_norm(h, ln_emb_g, ln_emb_b)
    pos = jnp.arange(h.shape[1])
    for l in range(DEPTH):
        h = hybrid_layer(h, pos, w_in[l], b_in[l], conv_w[l], conv_b[l], w_ra[l], b_ra[l], w_ri[l], b_ri[l],
                         lru_lambda[l], sinks[l], w_rnn_out[l], w_attn_out[l], w_o[l], b_o[l], ln_g[l], ln_b[l])
    return h[:, N_META:]
```

```python
import contextlib
import numpy as np
import ml_dtypes
import concourse.bass as bass
import concourse.mybir as mybir
from concourse.bass_utils import run_bass_kernel_spmd

F32 = mybir.dt.float32
BF16 = mybir.dt.bfloat16
AF = mybir.ActivationFunctionType
ALU = mybir.AluOpType

D = 2048
SEQ = 2048
NMETA = 16
NT = 1024
OFF_GR, OFF_Q, OFF_K, OFF_V, OFF_GA, OFF_G = 2048, 4096, 6144, 6400, 6656, 8704
D_IN = 12800
LN_EPS = 1e-5
ALPHA = 2.0 ** 0.25
ENGS = ("pe", "act", "dve", "pool", "sp")
DEBUG = False


class Prog:
    def __init__(self, nc):
        self.nc = nc
        self.q = {e: [] for e in ENGS}
        self.cnt = {e: 0 for e in ENGS}
        self.seen = {e: {} for e in ENGS}
        self.lastw = {}
        self.readers = {}
        self.dma_cnt = {}
        self.sem_names = set(ENGS)

    def _need(self, eng, tok, waits):
        if tok is None:
            return
        s, v = tok
        if s == eng and eng == "pe":
            return
        if self.seen[eng].get(s, 0) >= v:
            return
        waits[s] = max(waits.get(s, 0), v)

    def _emit_waits(self, eng, waits):
        for s, v in waits.items():
            self.seen[eng][s] = v
            self.q[eng].append(("wait", s, v))

    def _deps(self, eng, reads, writes):
        waits = {}
        for k in reads:
            self._need(eng, self.lastw.get(k), waits)
        for k in writes:
            self._need(eng, self.lastw.get(k), waits)
            for t in self.readers.get(k, ()):
                self._need(eng, t, waits)
        self._emit_waits(eng, waits)

    def _commit(self, tok, reads, writes):
        for k in reads:
            self.readers.setdefault(k, []).append(tok)
        for k in writes:
            self.lastw[k] = tok
            self.readers[k] = []

    def op(self, eng, fn, reads=(), writes=(), signal=True):
        reads = tuple(reads)
        writes = tuple(writes)
        self._deps(eng, reads, writes)
        tok = (eng, self.cnt[eng] + 1)
        self._commit(tok, reads, writes)
        if signal:
            self.cnt[eng] += 1
            self.q[eng].append(("op", fn, eng, 1))
        else:
            self.q[eng].append(("op", fn, None, 0))

    def dma(self, queue, sem, fn, reads=(), writes=()):
        reads = tuple(reads)
        writes = tuple(writes)
        self.sem_names.add(sem)
        self._deps(queue, reads, writes)
        if self.dma_cnt.get(sem, 0) > 0:
            w = {}
            self._need(queue, (sem, self.dma_cnt[sem]), w)
            self._emit_waits(queue, w)
        self.dma_cnt[sem] = self.dma_cnt.get(sem, 0) + 16
        tok = (sem, self.dma_cnt[sem])
        self._commit(tok, reads, writes)
        self.q[queue].append(("op", fn, sem, 16))

    def barrier(self):
        for e in ENGS:
            waits = {}
            for s in ENGS:
                if s != e:
                    self._need(e, (s, self.cnt[s]), waits)
            for s, v in self.dma_cnt.items():
                self._need(e, (s, v), waits)
            self._emit_waits(e, waits)

    def emit(self):
        nc = self.nc
        with contextlib.ExitStack() as st:
            sems = {}
            for s in sorted(self.sem_names):
                sems[s] = st.enter_context(nc.semaphore("sem_" + s))
            block = st.enter_context(nc.Block())

            def replay(e, name):
                for item in self.q[name]:
                    if item[0] == "wait":
                        e.wait_ge(sems[item[1]], item[2])
                    else:
                        _, fn, s, inc = item
                        ins = fn(e)
                        if s is not None:
                            ins.then_inc(sems[s], inc)

            @block.tensor
            def _(e):
                replay(e, "pe")

            @block.scalar
            def _(e):
                replay(e, "act")

            @block.vector
            def _(e):
                replay(e, "dve")

            @block.gpsimd
            def _(e):
                replay(e, "pool")

            @block.sync
            def _(e):
                replay(e, "sp")


class Region:
    def __init__(self, ap_f32, nwords):
        self.t = ap_f32
        self.n = nwords
        self.off = 0

    def reset(self):
        self.off = 0

    def take(self, shape, dt):
        n = int(np.prod(shape))
        esz = 2 if dt == BF16 else 4
        words = (n * esz + 3) // 4
        words = (words + 7) // 8 * 8
        assert self.off + words <= self.n, ("region overflow", self.off, words, self.n)
        v = self.t[:, self.off:self.off + words]
        self.off += words
        if dt == BF16:
            v = v.bitcast(BF16)
        v = v[:, 0:n]
        if len(shape) == 2:
            v = v.rearrange("p (a b) -> p a b", a=shape[0])
        elif len(shape) == 3:
            v = v.rearrange("p (a b c) -> p a b c", a=shape[0], b=shape[1])
        return v


def build_program():
    nc = bass.Bass("TRN2", target_bir_lowering=False)
    P = Prog(nc)

    def din(name, shape, dt=F32):
        return nc.dram_tensor(name, list(shape), dt, kind="ExternalInput").ap()

    x_d = din("x", [SEQ, D])
    meta_d = din("meta", [NMETA, D])
    w_in_d = din("w_in", [D, D_IN])
    w_ra_d = din("w_ra", [8, 256, 256])
    w_ri_d = din("w_ri", [8, 256, 256])
    w_rnn_d = din("w_rnn_out", [D, D])
    w_att_d = din("w_attn_out", [D, D])
    w_o_d = din("w_o", [D, D])
    bin_d = din("bin_cols", [128, 100])
    bk_d = din("bk_dup", [128, 4])
    convw_d = din("convw", [128, 64])
    convb_d = din("convb", [128, 16])
    bra_d = din("bra", [128, 16])
    bri_d = din("bri", [128, 16])
    lam_d = din("lam", [128, 16])
    ge_d = din("ge_cols", [128, 16])
    be_d = din("be_cols", [128, 16])
    bv_d = din("bv_bc", [128, 256])
    sink_d = din("sink_rows", [8, 512])
    gebc_d = din("ge_bc", [128, D])
    bebc_d = din("be_bc", [128, D])
    bobc_d = din("bo_bc", [128, D])
    lgbc_d = din("lg_bc", [128, D])
    lbbc_d = din("lb_bc", [128, D])
    ident_d = din("ident", [128, 128])
    rm_d = din("rotm", [128, 128])
    mc_d = din("mask_c", [128, 128], BF16)
    mp_d = din("mask_p", [128, 128], BF16)
    cos_d = din("cos_t", [128, NMETA + SEQ])
    sin_d = din("sin_t", [128, NMETA + SEQ])
    out_d = nc.dram_tensor("out", [SEQ, D], F32, kind="ExternalOutput").ap()

    dbg = {}
    if DEBUG:
        for nm, shp, dt in (("dbg_hT", [2, 128, 16, NMETA + NT], BF16), ("dbg_AT", [2, 128, 16, NT], BF16), ("dbg_OG", [2, 128, 16, NT], BF16),
                            ("dbg_MX", [2, 128, 16, NT], BF16), ("dbg_out", [2, 128, 8, D], F32), ("dbg_KT", [2, 128, 4, NMETA + 128 + NT], BF16),
                            ("dbg_Vt", [2, 128, 9, 4, 64], BF16)):
            dbg[nm] = nc.dram_tensor(nm, shp, dt, kind="ExternalOutput").ap()
    w_in_v = w_in_d.rearrange("(kc p) c -> p kc c", p=128)
    w_rnn_v = w_rnn_d.rearrange("(kc p) c -> p kc c", p=128)
    w_att_v = w_att_d.rearrange("(kc p) c -> p kc c", p=128)
    w_o_v = w_o_d.rearrange("(kc p) c -> p kc c", p=128)

    st = contextlib.ExitStack()
    with st:
        def sb(name, shape, dt):
            return st.enter_context(nc.sbuf_tensor("s_" + name, list(shape), dt))

        def psb(name):
            return st.enter_context(nc.psum_tensor("ps_" + name, [128, 512], F32))

        ident = sb("ident", [128, 128], F32)
        rotm = sb("rotm", [128, 128], F32)
        mask_c = sb("mask_c", [128, 128], BF16)
        mask_p = sb("mask_p", [128, 128], BF16)
        ones = sb("ones", [128, 64], BF16)
        bin_c = sb("bin_c", [128, 100], F32)
        hbin_c = sb("hbin_c", [128, 100], F32)
        bk_c = sb("bk_c", [128, 4], F32)
        convw = sb("convw", [128, 64], F32)
        convb = sb("convb", [128, 16], F32)
        hbra = sb("hbra", [128, 16], F32)
        hbri = sb("hbri", [128, 16], F32)
        lam = sb("lam", [128, 16], F32)
        cl = sb("cl", [128, 16], F32)
        hcl = sb("hcl", [128, 16], F32)
        ge_c = sb("ge_c", [128, 16], F32)
        be_c = sb("be_c", [128, 16], F32)
        bv_bc = sb("bv_bc", [128, 256], F32)
        eps_t = sb("eps_t", [128, 1], F32)
        one_t = sb("one_t", [128, 1], F32)
        halo = sb("halo", [128, 16, 3], F32)
        hlast = sb("hlast", [128, 16], F32)
        stats = sb("stats", [128, 4, 6], F32)
        mv = sb("mv", [128, 2], F32)
        rstd = sb("rstd", [128, 1], F32)
        stats_b = sb("stats_b", [128, 4, 6], F32)
        ssum = sb("ssum", [128, 4], F32)
        mv_b = sb("mv_b", [128, 2], F32)
        rstd_b = sb("rstd_b", [128, 1], F32)
        sinkst = sb("sinkst", [128, 512], F32)
        ptm = [sb(f"ptm{v}", [128, 512], BF16) for v in range(8)]
        vm = sb("vm", [128, 4, 64], BF16)
        wz_all = sb("wz_all", [128, 3, 16, 128], BF16)
        wz = [wz_all[:, i] for i in range(3)]
        wv = sb("wv", [128, 16, 256], BF16)
        wg = [sb(f"wg{i}", [128, 2, 2, 256], BF16) for i in range(2)]
        RH_W = 8320
        RAO_W = 16384
        RW_W = 16384
        rh_t = sb("rh", [128, RH_W], F32)
        rao_t = sb("rao", [128, RAO_W], F32)
        rw_t = sb("rw", [128, RW_W], F32)
        RH = Region(rh_t[:], RH_W)
        RAO = Region(rao_t[:], RAO_W)
        RW = Region(rw_t[:], RW_W)

        Zb = [psb(f"zb{i}") for i in range(3)]
        Sb = [psb(f"sbk{i}") for i in range(3)]
        OB = psb("ob")
        DB = psb("db")
        zctr = [0]

        zpool = [[(Zb[0], "Z0"), (Zb[1], "Z1"), (Zb[2], "Z2")]]

        def next_z():
            pool = zpool[0]
            i = zctr[0] % len(pool)
            zctr[0] += 1
            return pool[i]

        sctr = [0]

        def next_s():
            i = sctr[0] % 3
            sctr[0] += 1
            return Sb[i], f"S{i}"

        dctr = [0]

        def dsem():
            dctr[0] += 1
            return f"d{dctr[0] % 6}"

        def ld(dst, src, key, q="sp"):
            P.dma(q, "c" + key, lambda e: e.dma_start(out=dst, in_=src), writes=[key])

        ld(ident[:], ident_d, "ident")
        ld(ge_c[:], ge_d, "ge_c")
        ld(be_c[:], be_d, "be_c")
        P.op("dve", lambda e: e.memset(eps_t[:], LN_EPS), writes=["eps_t"])

        def late_setup():
            ld(rotm[:], rm_d, "rotm")
            ld(mask_c[:], mc_d, "mask_c")
            ld(mask_p[:], mp_d, "mask_p")
            ld(bin_c[:], bin_d, "bin_c")
            ld(bk_c[:], bk_d, "bk_c")
            ld(convw[:], convw_d, "convw")
            ld(convb[:], convb_d, "convb")
            ld(hbra[:], bra_d, "hbra")
            ld(hbri[:], bri_d, "hbri")
            ld(lam[:], lam_d, "lam")
            ld(bv_bc[:], bv_d, "bv_bc")
            P.op("dve", lambda e: e.memset(ones[:], 1.0), writes=["ones"])
            P.op("dve", lambda e: e.memset(one_t[:], 1.0), writes=["one_t"])
            P.op("dve", lambda e: e.memset(halo[:], 0.0), writes=["halo"])
            P.op("dve", lambda e: e.memset(hlast[:], 0.0), writes=["hlast"])
            P.op("dve", lambda e: e.memset(vm[:], 0.0), writes=["vm"])
            P.op("dve", lambda e: e.tensor_scalar(out=hbin_c[:], in0=bin_c[:], scalar1=0.5, scalar2=None, op0=ALU.mult), reads=["bin_c"], writes=["hbin_c"])
            P.op("dve", lambda e: e.tensor_scalar(out=hbra[:], in0=hbra[:], scalar1=0.5, scalar2=None, op0=ALU.mult), reads=["hbra"], writes=["hbra"])
            P.op("dve", lambda e: e.tensor_scalar(out=hbri[:], in0=hbri[:], scalar1=0.5, scalar2=None, op0=ALU.mult), reads=["hbri"], writes=["hbri"])
            P.op("act", lambda e: e.activation(out=cl[:], in_=lam[:], func=AF.Exp, scale=-1.0), reads=["lam"], writes=["cl"])
            P.op("dve", lambda e: e.tensor_scalar(out=cl[:], in0=cl[:], scalar1=1.0, scalar2=None, op0=ALU.add), reads=["cl"], writes=["cl"])
            P.op("act", lambda e: e.activation(out=cl[:], in_=cl[:], func=AF.Ln), reads=["cl"], writes=["cl"])
            P.op("dve", lambda e: e.tensor_scalar(out=hcl[:], in0=cl[:], scalar1=-4.0, scalar2=None, op0=ALU.mult), reads=["cl"], writes=["hcl"])
            P.op("dve", lambda e: e.tensor_scalar(out=cl[:], in0=cl[:], scalar1=-8.0, scalar2=None, op0=ALU.mult), reads=["cl", "hcl"], writes=["cl"])

        hT = RH.take((16, NMETA + NT), BF16)
        AT = RAO.take((16, NT), BF16)
        OG = RAO.take((16, NT), BF16)
        KT = RW.take((4, NMETA + 128 + NT), BF16)
        Vt = RW.take((9, 4, 64), BF16)
        RW_MARK = RW.off

        wctr = [0]

        def z_chunks(ps, with_meta):
            if ps == 0:
                ch = [(NMETA, 512), (NMETA + 512, 512)]
                if with_meta:
                    ch = [(0, NMETA)] + ch
                return ch
            return [(0, 512), (512, 512)]

        def ht_keys(ps, cs, n):
            if ps == 0:
                if cs == 0:
                    return ["hT_m"]
                b0 = (cs - NMETA) // 128
            else:
                b0 = cs // 128
            return [f"hT_{b0 + i}" for i in range((n + 127) // 128)]

        def do_pass(ps):
            OFF = NMETA if ps == 0 else 0
            W = OFF + NT
            pfx = f"p{ps}_"
            P.barrier()
            zpool[0] = [(Zb[0], "Z0"), (Zb[1], "Z1"), (Zb[2], "Z2")]
            RW.off = RW_MARK
            xbuf = [RW.take((D,), F32) for _ in range(2)]
            xnb = [RW.take((D,), F32) for _ in range(2)]
            blocks = ([("m", meta_d, NMETA, 0)] if ps == 0 else []) + [
                (str(b), x_d[ps * NT + b * 128: ps * NT + (b + 1) * 128, :], 128, OFF + b * 128) for b in range(8)]
            stA, stB = [], []
            lnbanks = [(Zb[0], "Z0"), (Zb[1], "Z1"), (Zb[2], "Z2"), (Sb[0], "S0"), (Sb[1], "S1"), (Sb[2], "S2")]
            lnctr = [0]
            for bi, (bn, src, n, col0) in enumerate(blocks):
                def A_(bi=bi, bn=bn, src=src, n=n, col0=col0):
                    xb = xbuf[bi % 2]
                    xn = xnb[bi % 2]
                    kx = pfx + f"xb{bi % 2}"
                    kn = pfx + f"xn{bi % 2}"
                    st_, mv_, rs_ = (stats, mv, rstd) if bi % 2 == 0 else (stats_b, mv_b, rstd_b)
                    sfx = "" if bi % 2 == 0 else "_b"
                    P.dma("sp", dsem(), (lambda xb, src, n: lambda e: e.dma_start(out=xb[0:n, :], in_=src))(xb, src, n), writes=[kx])
                    for j in range(4):
                        P.op("dve", (lambda xb, n, j, st_: lambda e: e.bn_stats(out=st_[0:n, j, :], in_=xb[0:n, j * 512:(j + 1) * 512]))(xb, n, j, st_), reads=[kx], writes=[f"stats{j}" + sfx])
                    P.op("dve", (lambda n, st_, mv_: lambda e: e.bn_aggr(out=mv_[0:n, :], in_=st_[0:n].rearrange("p a b -> p (a b)")))(n, st_, mv_), reads=[f"stats{j}" + sfx for j in range(4)], writes=["mv" + sfx])
                    P.op("act", (lambda n, mv_, rs_: lambda e: e.activation(out=rs_[0:n, :], in_=mv_[0:n, 1:2], func=AF.Sqrt, bias=eps_t[0:n, :]))(n, mv_, rs_), reads=["mv" + sfx, "eps_t"], writes=["rstd" + sfx])
                    P.op("dve", (lambda n, rs_: lambda e: e.reciprocal(out=rs_[0:n, :], in_=rs_[0:n, :]))(n, rs_), reads=["rstd" + sfx], writes=["rstd" + sfx])
                    P.op("dve", (lambda xb, xn, n, mv_, rs_: lambda e: e.tensor_scalar(out=xn[0:n, :], in0=xb[0:n, :], scalar1=mv_[0:n, 0:1], scalar2=rs_[0:n, 0:1], op0=ALU.subtract, op1=ALU.mult))(xb, xn, n, mv_, rs_), reads=[kx, "mv" + sfx, "rstd" + sfx], writes=[kn])
                def B_(bi=bi, bn=bn, src=src, n=n, col0=col0):
                    xn = xnb[bi % 2]
                    kn = pfx + f"xn{bi % 2}"
                    for g4 in range(4):
                        zb, zk = lnbanks[lnctr[0] % 6]
                        lnctr[0] += 1
                        for j in range(4):
                            kc = g4 * 4 + j
                            P.op("pe", (lambda zb, xn, n, j, kc: lambda e: e.transpose(out=zb[:, j * 128:j * 128 + n], in_=xn[0:n, kc * 128:(kc + 1) * 128], identity=ident[0:n, 0:n]))(zb, xn, n, j, kc), reads=[kn, "ident"], writes=[zk], signal=(j == 3))
                        for j in range(4):
                            kc = g4 * 4 + j
                            if g4 != 3:
                                P.op("act", (lambda zb, n, j, kc, col0: lambda e: e.activation(out=hT[:, kc, col0:col0 + n], in_=zb[:, j * 128:j * 128 + n], func=AF.Identity, scale=ge_c[:, kc:kc + 1], bias=be_c[:, kc:kc + 1]))(zb, n, j, kc, col0), reads=[zk, "ge_c", "be_c"], writes=[f"hT_{bn}_{kc}"])
                            else:
                                P.op("dve", (lambda zb, n, j, kc, col0: lambda e: e.tensor_scalar(out=hT[:, kc, col0:col0 + n], in0=zb[:, j * 128:j * 128 + n], scalar1=ge_c[:, kc:kc + 1], scalar2=be_c[:, kc:kc + 1], op0=ALU.mult, op1=ALU.add))(zb, n, j, kc, col0), reads=[zk, "ge_c", "be_c"], writes=[f"hT_{bn}_{kc}"])
                stA.append(A_)
                stB.append(B_)
            stA[0]()
            for bi in range(len(blocks)):
                if bi + 1 < len(blocks):
                    stA[bi + 1]()
                stB[bi]()
            if ps == 0:
                late_setup()
            def htk(ps, cs, n):
                ks = []
                for b in ht_keys(ps, cs, n):
                    bn = b[3:]
                    ks += [f"hT_{bn}_{kc}" for kc in range(16)]
                return ks

            P.barrier()
            RW.off = RW_MARK
            cosT = RW.take((W,), F32)
            sinT = RW.take((W,), F32)
            p0 = 0 if ps == 0 else NMETA + NT
            P.dma("sp", dsem(), lambda e: e.dma_start(out=cosT, in_=cos_d[:, p0:p0 + W]), writes=[pfx + "cosT"])
            P.dma("sp", dsem(), lambda e: e.dma_start(out=sinT, in_=sin_d[:, p0:p0 + W]), writes=[pfx + "sinT"])
            Qg = [RW.take((4, NT), BF16) for _ in range(2)]
            PT = [[RW.take((512,), BF16) for _ in range(2)] for _ in range(2)]
            kz = [RW.take((512,), F32) for _ in range(2)]
            t1b = [RW.take((512,), F32) for _ in range(2)]
            t2b = [RW.take((512,), F32) for _ in range(2)]
            wv_f32a = wv[:].rearrange("p a b -> p (a b)").bitcast(F32)
            tgb = [RW.take((512,), F32), wv_f32a[:, 0:512]]
            xgb = [RW.take((512,), F32), wv_f32a[:, 512:1024]]
            gactr = [0]
            rden = RW.take((512,), F32)
            ogt = RW.take((512,), F32)

            P.dma("pool", "wv", lambda e: e.dma_start(out=wv[:], in_=w_in_v[:, :, OFF_V:OFF_V + 256]), writes=["wv"])
            for bi, (bn, src, n, col0) in enumerate(blocks):
                zb, zk = next_z()
                for kc in range(16):
                    P.op("pe", (lambda zb, n, kc, col0: lambda e: e.matmul(zb[0:n, 0:256], lhsT=hT[:, kc, col0:col0 + n], rhs=wv[:, kc, :], start=(kc == 0), stop=(kc == 15)))(zb, n, kc, col0), reads=["wv", f"hT_{bn}_{kc}"], writes=[zk], signal=(kc == 15))
                if bn == "m":
                    P.op("dve", (lambda zb: lambda e: e.tensor_tensor(out=vm[0:16, :, :], in0=zb[0:16, 0:256].rearrange("p (a b) -> p a b", a=4), in1=bv_bc[0:16, :].rearrange("p (a b) -> p a b", a=4), op=ALU.add))(zb), reads=[zk, "bv_bc", "vm"], writes=["vm"])
                else:
                    slot = int(bn) + 1
                    P.op("dve", (lambda zb, slot: lambda e: e.tensor_tensor(out=Vt[:, slot, :, :], in0=zb[:, 0:256].rearrange("p (a b) -> p a b", a=4), in1=bv_bc[:, :].rearrange("p (a b) -> p a b", a=4), op=ALU.add))(zb, slot), reads=[zk, "bv_bc"], writes=[pfx + f"Vt{slot}"])

            if ps == 0:
                for v in range(8):
                    P.op("dve", (lambda v: lambda e: e.memset(ptm[v][:], 0.0))(v), writes=[f"ptm{v}"])
                    P.dma("sp", "csink", (lambda v: lambda e: e.dma_start(out=sinkst[32:33, :], in_=sink_d[v:v + 1, :]))(v), writes=["sinkst"])
                    P.op("act", (lambda v: lambda e: e.activation(out=ptm[v][32:33, :], in_=sinkst[32:33, :], func=AF.Exp))(v), reads=["sinkst"], writes=[f"ptm{v}"])
            rot_q = []
            att_q = []
            PUMP = [2]

            def pump(k):
                while rot_q:
                    rot_q.pop(0)()
                for _ in range(k):
                    if att_q:
                        att_q.pop(0)()

            def pump_all():
                while rot_q or att_q:
                    pump(4)

            def z_tile(load_fn, rhs_fn, rhs_keys_fn, chunks, epi):
                i = wctr[0] % 3
                wctr[0] += 1
                w = wz[i]
                wk = f"wz{i}"
                load_fn(w, wk)
                for (cs, n) in chunks:
                    zb, zk = next_z()
                    for kc in range(16):
                        P.op("pe", (lambda zb, n, kc, cs, w: lambda e: e.matmul(zb[:, 0:n], lhsT=w[:, kc, :], rhs=rhs_fn(kc, cs, n), start=(kc == 0), stop=(kc == 15)))(zb, n, kc, cs, w), reads=[wk] + rhs_keys_fn(kc, cs, n), writes=[zk], signal=(kc == 15))
                    pump(PUMP[0])
                    epi(zb, zk, cs, n)

            def load_win(c0):
                def f(w, wk):
                    P.dma("pool", wk, lambda e: e.dma_start(out=w[:], in_=w_in_v[:, :, c0:c0 + 128]), writes=[wk])
                return f

            def load_kdup(kh):
                def f(w, wk):
                    c0 = OFF_K + 64 * kh
                    P.dma("pool", wk, lambda e: e.dma_start(out=w[:, :, 0:64], in_=w_in_v[:, :, c0:c0 + 64]), writes=[wk])
                    P.dma("pool", wk + "b", lambda e: e.dma_start(out=w[:, :, 64:128], in_=w_in_v[:, :, c0:c0 + 64]), writes=[wk])
                return f

            def rhs_h(kc, cs, n):
                return hT[:, kc, cs:cs + n]

            def rhs_h_keys(kc, cs, n):
                return [f"hT_{b[3:]}_{kc}" for b in ht_keys(ps, cs, n)]

            ectr = [0]

            def rope_epi(bias_ap, dst_fn, dst_key_fn):
                def epi(zb, zk, cs, n):
                    i = ectr[0] % 2
                    ectr[0] += 1
                    kzb, t1, t2 = kz[i], t1b[i], t2b[i]
                    P.op("act", lambda e: e.activation(out=kzb[:, 0:n], in_=zb[:, 0:n], func=AF.Identity, bias=bias_ap), reads=[zk, "bin_c", "bk_c"], writes=[pfx + f"kz{i}"])

                    def part2():
                        sbk, sk = next_s()
                        P.op("pe", lambda e: e.matmul(sbk[:, 0:n], lhsT=rotm[:], rhs=kzb[:, 0:n], start=True, stop=True), reads=[pfx + f"kz{i}", "rotm"], writes=[sk])
                        P.op("dve", lambda e: e.tensor_tensor(out=t1[:, 0:n], in0=kzb[:, 0:n], in1=cosT[:, cs:cs + n], op=ALU.mult), reads=[pfx + f"kz{i}", pfx + "cosT"], writes=[pfx + f"t1{i}"])
                        P.op("dve", lambda e: e.tensor_tensor(out=t2[:, 0:n], in0=sbk[:, 0:n], in1=sinT[:, cs:cs + n], op=ALU.mult), reads=[sk, pfx + "sinT"], writes=[pfx + f"t2{i}"])
                        P.op("dve", lambda e: e.tensor_tensor(out=dst_fn(cs, n), in0=t1[:, 0:n], in1=t2[:, 0:n], op=ALU.add), reads=[pfx + f"t1{i}", pfx + f"t2{i}"], writes=[dst_key_fn(cs, n)])
                    rot_q.append(part2)
                return epi

            def kcol(cs):
                return cs if (ps == 0 and cs < NMETA) else NMETA + 128 + (cs - OFF)

            if ps == 1:
                pass
            for kh in range(4):
                z_tile(load_kdup(kh), rhs_h, rhs_h_keys, z_chunks(ps, True),
                       rope_epi(bk_c[:, kh:kh + 1], (lambda kh: lambda cs, n: KT[:, kh, kcol(cs):kcol(cs) + n])(kh), (lambda kh: lambda cs, n: pfx + f"KT{kh}_{kcol(cs)}")(kh)))

            def kt_keys(kh, c0, n):
                ks = []
                if c0 < NMETA:
                    return [f"p0_KT{kh}_0"]
                if c0 < NMETA + 128:
                    return [f"KTprev{kh}"]
                cc = c0 - (NMETA + 128)
                base = (cc // 512) * 512
                return [pfx + f"KT{kh}_{NMETA + 128 + base}"]

            def attention_items(kh):
                qb = Qg[kh % 2]

                def mkA(n, half):
                    def A():
                        g = 8 * ps + n
                        q0 = n * 128
                        kcur = NMETA + 128 + n * 128
                        kprev = kcur - 128
                        r0 = half * 64
                        v = kh * 2 + half
                        ptc, ptp = PT[half]
                        qkeys = [pfx + f"Qg{kh % 2}_{i}_{(q0 // 512) * 512}" for i in range(4)]
                        s0, k0 = next_s()
                        P.op("pe", lambda e: e.matmul(s0[:, :].rearrange("p (a b) -> p a b", a=4), lhsT=KT[r0:r0 + 64, kh, kcur:kcur + 128], rhs=qb[r0:r0 + 64, :, q0:q0 + 128], start=True, stop=True), reads=qkeys + kt_keys(kh, kcur, 128), writes=[k0])
                        P.op("act", lambda e: e.activation(out=ptc, in_=s0[:, :], func=AF.Exp, scale=0.125), reads=[k0], writes=[pfx + f"ptc{half}"])
                        P.op("dve", lambda e: e.tensor_tensor(out=ptc.rearrange("p (a b) -> p a b", a=4), in0=ptc.rearrange("p (a b) -> p a b", a=4), in1=mask_c[:, :].unsqueeze(1).broadcast_to([128, 4, 128]), op=ALU.mult), reads=[pfx + f"ptc{half}", "mask_c"], writes=[pfx + f"ptc{half}"])
                        if g > 0:
                            s1, k1 = next_s()
                            P.op("pe", lambda e: e.matmul(s1[:, :].rearrange("p (a b) -> p a b", a=4), lhsT=KT[r0:r0 + 64, kh, kprev:kprev + 128], rhs=qb[r0:r0 + 64, :, q0:q0 + 128], start=True, stop=True), reads=qkeys + kt_keys(kh, kprev, 128), writes=[k1])
                            P.op("act", lambda e: e.activation(out=ptp, in_=s1[:, :], func=AF.Exp, scale=0.125), reads=[k1], writes=[pfx + f"ptp{half}"])
                            P.op("dve", lambda e: e.tensor_tensor(out=ptp.rearrange("p (a b) -> p a b", a=4), in0=ptp.rearrange("p (a b) -> p a b", a=4), in1=mask_p[:, :].unsqueeze(1).broadcast_to([128, 4, 128]), op=ALU.mult), reads=[pfx + f"ptp{half}", "mask_p"], writes=[pfx + f"ptp{half}"])
                        s2, k2 = next_s()
                        P.op("pe", lambda e: e.matmul(s2[0:16, :].rearrange("p (a b) -> p a b", a=4), lhsT=KT[r0:r0 + 64, kh, 0:NMETA], rhs=qb[r0:r0 + 64, :, q0:q0 + 128], start=True, stop=True), reads=qkeys + kt_keys(kh, 0, 16), writes=[k2])
                        P.op("act", lambda e: e.activation(out=ptm[v][0:16, :], in_=s2[0:16, :], func=AF.Exp, scale=0.125), reads=[k2], writes=[f"ptm{v}"])
                    return A

                def mkB(n, half):
                    def B():
                        g = 8 * ps + n
                        q0 = n * 128
                        r0 = half * 64
                        v = kh * 2 + half
                        ptc, ptp = PT[half]
                        seq = [(Vt[:, n + 1, kh, :], ones[:, :], ptc, 128, [pfx + f"ptc{half}", vt_key(n + 1)])]
                        if g > 0:
                            seq.append((Vt[:, n, kh, :], ones[:, :], ptp, 128, [pfx + f"ptp{half}", vt_key(n)]))
                        seq.append((vm[0:33, kh, :], ones[0:33, :], ptm[v][0:33, :], 33, [f"ptm{v}", "vm"]))
                        ns = len(seq)
                        for si, (vl, ol, pr, kk, rk) in enumerate(seq):
                            P.op("pe", lambda e, vl=vl, pr=pr, si=si: e.matmul(OB[r0:r0 + 64, :], lhsT=vl, rhs=pr, start=(si == 0), stop=(si == ns - 1)), reads=rk, writes=[f"OB{half}"], signal=(si == ns - 1))
                        for si, (vl, ol, pr, kk, rk) in enumerate(seq):
                            P.op("pe", lambda e, ol=ol, pr=pr, si=si: e.matmul(DB[r0:r0 + 64, :], lhsT=ol, rhs=pr, start=(si == 0), stop=(si == ns - 1)), reads=rk + ["ones"], writes=[f"DB{half}"], signal=(si == ns - 1))
                        if half == 1:
                            P.op("dve", lambda e: e.reciprocal(out=rden, in_=DB[:, :]), reads=["DB0", "DB1"], writes=[pfx + "rden"])
                            P.op("dve", lambda e: e.tensor_tensor(out=ogt, in0=OB[:, :], in1=rden, op=ALU.mult), reads=["OB0", "OB1", pfx + "rden"], writes=[pfx + "ogt"])
                            ogk = [pfx + f"OG{4 * kh + i}_{(q0 // 512) * 512}" for i in range(4)]
                            P.op("dve", lambda e: e.tensor_tensor(out=OG[:, 4 * kh:4 * kh + 4, q0:q0 + 128], in0=ogt.rearrange("p (a b) -> p a b", a=4), in1=OG[:, 4 * kh:4 * kh + 4, q0:q0 + 128], op=ALU.mult), reads=[pfx + "ogt"] + ogk, writes=ogk)
                    return B

                units = [(n, h) for n in range(8) for h in range(2)]
                As = [mkA(n, h) for (n, h) in units]
                Bs = [mkB(n, h) for (n, h) in units]
                items = [As[0], As[1]]
                for i in range(16):
                    items.append(Bs[i])
                    if i + 2 < 16:
                        items.append(As[i + 2])
                return items

            def vt_key(slot):
                if slot == 0:
                    return "Vtprev"
                return pfx + f"Vt{slot}"

            def ga_epi(j):
                def epi(zb, zk, cs, n):
                    i = gactr[0] % 2
                    gactr[0] += 1
                    tg, xg = tgb[i], xgb[i]
                    ct = OFF_GA // 128 + j
                    t0 = cs - OFF
                    P.op("act", lambda e: e.activation(out=tg[:, 0:n], in_=zb[:, 0:n], func=AF.Tanh, scale=0.5, bias=hbin_c[:, ct:ct + 1]), reads=[zk, "hbin_c"], writes=[pfx + f"tg{i}"])
                    P.op("act", lambda e: e.activation(out=xg[:, 0:n], in_=zb[:, 0:n], func=AF.Identity, bias=bin_c[:, ct:ct + 1]), reads=[zk, "bin_c"], writes=[pfx + f"xg{i}"])
                    P.op("dve", lambda e: e.scalar_tensor_tensor(out=OG[:, j, t0:t0 + n], in0=tg[:, 0:n], scalar=1.0, in1=xg[:, 0:n], op0=ALU.add, op1=ALU.mult), reads=[pfx + f"tg{i}", pfx + f"xg{i}"], writes=[pfx + f"OG{j}_{t0}"])
                return epi

            for kh in range(4):
                for i4 in range(4):
                    j = 4 * kh + i4
                    z_tile(load_win(OFF_GA + 128 * j), rhs_h, rhs_h_keys, z_chunks(ps, False), ga_epi(j))
                for i4 in range(4):
                    j = 4 * kh + i4
                    ct = OFF_Q // 128 + j
                    z_tile(load_win(OFF_Q + 128 * j), rhs_h, rhs_h_keys, z_chunks(ps, False),
                           rope_epi(bin_c[:, ct:ct + 1], (lambda kh, i4: lambda cs, n: Qg[kh % 2][:, i4, cs - OFF:cs - OFF + n])(kh, i4), (lambda kh, i4: lambda cs, n: pfx + f"Qg{kh % 2}_{i4}_{cs - OFF}")(kh, i4)))
                att_q.extend(attention_items(kh))
            pump_all()
            if ps == 0:
                for kh in range(4):
                    P.op("dve", (lambda kh: lambda e: e.tensor_copy(out=KT[:, kh, NMETA:NMETA + 128], in_=KT[:, kh, NMETA + NT:NMETA + NT + 128]))(kh), reads=[f"p0_KT{kh}_{NMETA + 128 + 512}"], writes=[f"KTprev{kh}"])
                P.op("dve", lambda e: e.tensor_copy(out=Vt[:, 0, :, :], in_=Vt[:, 8, :, :]), reads=["p0_Vt8"], writes=["Vtprev"])

            P.barrier()
            zpool[0] = [(Zb[0], "Z0"), (Zb[1], "Z1"), (Zb[2], "Z2")] + [(OB, "OBz"), (DB, "DBz")]
            RW.off = RW_MARK
            xr = RW.take((2, 3 + W), F32)
            xc = RW.take((2, W), F32)
            xcb = RW.take((2, W), BF16)
            wv_f32 = wv[:].rearrange("p a b -> p (a b)").bitcast(F32)
            trb = [RW.take((W,), F32), wv_f32[:, 0:W]]
            tib = [RW.take((W,), F32) for _ in range(2)]
            a2b = [RW.take((W,), F32) for _ in range(2)]
            hb = [RW.take((W,), F32) for _ in range(2)]
            tg2 = [KT[:, 2, NMETA + 128:NMETA + 128 + NT].bitcast(F32), KT[:, 0, NMETA + 128:NMETA + 128 + NT].bitcast(F32)]
            xg2 = [KT[:, 3, NMETA + 128:NMETA + 128 + NT].bitcast(F32), KT[:, 1, NMETA + 128:NMETA + 128 + NT].bitcast(F32)]
            grctr = [0]

            def xr_epi(ft):
                f2 = ft % 2
                def epi(zb, zk, cs, n):
                    P.op("act", lambda e: e.activation(out=xr[:, f2, 3 + cs:3 + cs + n], in_=zb[:, 0:n], func=AF.Identity, bias=bin_c[:, ft:ft + 1]), reads=[zk, "bin_c"], writes=[pfx + f"xr{f2}_{cs}"])
                return epi

            def gr_epi(ft):
                f2 = ft % 2
                def epi(zb, zk, cs, n):
                    i = grctr[0] % 2
                    grctr[0] += 1
                    tg, xg = tg2[i], xg2[i]
                    ct = OFF_GR // 128 + ft
                    t0 = cs - OFF
                    P.op("act", lambda e: e.activation(out=tg[:, 0:n], in_=zb[:, 0:n], func=AF.Tanh, scale=0.5, bias=hbin_c[:, ct:ct + 1]), reads=[zk, "hbin_c"], writes=[pfx + f"tgr{i}"])
                    P.op("act", lambda e: e.activation(out=xg[:, 0:n], in_=zb[:, 0:n], func=AF.Identity, bias=bin_c[:, ct:ct + 1]), reads=[zk, "bin_c"], writes=[pfx + f"xgr{i}"])
                    P.op("dve", lambda e: e.scalar_tensor_tensor(out=tg[:, 0:n], in0=tg[:, 0:n], scalar=1.0, in1=xg[:, 0:n], op0=ALU.add, op1=ALU.mult), reads=[pfx + f"tgr{i}", pfx + f"xgr{i}"], writes=[pfx + f"tgr{i}"])
                    P.op("dve", lambda e: e.tensor_tensor(out=AT[:, ft, t0:t0 + n], in0=tg[:, 0:n], in1=hb[f2][:, cs:cs + n], op=ALU.mult), reads=[pfx + f"tgr{i}", pfx + f"h{f2}"], writes=[pfx + f"AT{ft}_{t0}"])
                return epi

            chunks_m = z_chunks(ps, True)
            for nb in range(8):
                wgi = wg[nb % 2]
                wgk = f"wg{nb % 2}"
                P.dma("pool", wgk + "r", (lambda wgi, nb: lambda e: e.dma_start(out=wgi[:, 0, :, :], in_=w_ra_d[nb].rearrange("(kc p) d -> p kc d", p=128)))(wgi, nb), writes=[wgk + "r"])
                P.dma("pool", wgk + "i", (lambda wgi, nb: lambda e: e.dma_start(out=wgi[:, 1, :, :], in_=w_ri_d[nb].rearrange("(kc p) d -> p kc d", p=128)))(wgi, nb), writes=[wgk + "i"])
                for f2 in range(2):
                    ft = 2 * nb + f2
                    P.op("dve", (lambda f2, ft: lambda e: e.tensor_copy(out=xr[:, f2, 0:3], in_=halo[:, ft, :]))(f2, ft), reads=["halo"], writes=[pfx + f"xrh{f2}"])
                    z_tile(load_win(128 * ft), rhs_h, rhs_h_keys, chunks_m, xr_epi(ft))
                for f2 in range(2):
                    ft = 2 * nb + f2
                    xk = [pfx + f"xr{f2}_{cs}" for (cs, n) in chunks_m] + [pfx + f"xrh{f2}"]
                    P.op("dve", (lambda f2, ft: lambda e: e.tensor_scalar(out=xc[:, f2, :], in0=xr[:, f2, 3:3 + W], scalar1=convw[:, 4 * ft:4 * ft + 1], scalar2=convb[:, ft:ft + 1], op0=ALU.mult, op1=ALU.add))(f2, ft), reads=xk + ["convw", "convb"], writes=[pfx + f"xc{f2}"])
                    for k in range(1, 4):
                        P.op("dve", (lambda f2, ft, k: lambda e: e.scalar_tensor_tensor(out=xc[:, f2, :], in0=xr[:, f2, 3 - k:3 - k + W], scalar=convw[:, 4 * ft + k:4 * ft + k + 1], in1=xc[:, f2, :], op0=ALU.mult, op1=ALU.add))(f2, ft, k), reads=xk + [pfx + f"xc{f2}"], writes=[pfx + f"xc{f2}"])
                    P.op("dve", (lambda f2, ft: lambda e: e.tensor_copy(out=halo[:, ft, :], in_=xr[:, f2, W:W + 3]))(f2, ft), reads=xk, writes=["halo"])
                    P.op("dve", (lambda f2: lambda e: e.tensor_copy(out=xcb[:, f2, :], in_=xc[:, f2, :]))(f2), reads=[pfx + f"xc{f2}"], writes=[pfx + f"xcb{f2}"])
                if nb > 0:
                    for f2 in range(2):
                        ftp = 2 * (nb - 1) + f2
                        z_tile(load_win(OFF_GR + 128 * ftp), rhs_h, rhs_h_keys, z_chunks(ps, False), gr_epi(ftp))
                for f2 in range(2):
                    ft = 2 * nb + f2
                    tr, ti = trb[f2], tib[f2]
                    for gi, (tdst, hbias, tkey) in enumerate(((tr, hbra, "tr"), (ti, hbri, "ti"))):
                        for (cs, n) in chunks_m:
                            sbk, sk = next_s()
                            for kc2 in range(2):
                                P.op("pe", (lambda sbk, n, gi, kc2, f2, cs, wgi: lambda e: e.matmul(sbk[:, 0:n], lhsT=wgi[:, gi, kc2, f2 * 128:(f2 + 1) * 128], rhs=xcb[:, kc2, cs:cs + n], start=(kc2 == 0), stop=(kc2 == 1)))(sbk, n, gi, kc2, f2, cs, wgi), reads=[wgk + ("r" if gi == 0 else "i"), pfx + f"xcb{kc2}"], writes=[sk], signal=(kc2 == 1))
                            P.op("act", (lambda sbk, n, cs, tdst, hbias, ft: lambda e: e.activation(out=tdst[:, cs:cs + n], in_=sbk[:, 0:n], func=AF.Tanh, scale=0.5, bias=hbias[:, ft:ft + 1]))(sbk, n, cs, tdst, hbias, ft), reads=[sk, "hbra", "hbri"], writes=[pfx + f"{tkey}{f2}_{cs}"])
                for f2 in range(2):
                    ft = 2 * nb + f2
                    tr = trb[f2]
                    a2 = a2b[f2]
                    trk = [pfx + f"tr{f2}_{cs}" for (cs, n) in chunks_m]
                    P.op("act", (lambda a2, tr, ft: lambda e: e.activation(out=a2, in_=tr, func=AF.Exp, scale=cl[:, ft:ft + 1], bias=cl[:, ft:ft + 1]))(a2, tr, ft), reads=trk + ["cl"], writes=[pfx + f"a2{f2}"])
                    P.op("act", (lambda tr, ft: lambda e: e.activation(out=tr, in_=tr, func=AF.Exp, scale=hcl[:, ft:ft + 1], bias=hcl[:, ft:ft + 1]))(tr, ft), reads=trk + ["hcl", pfx + f"a2{f2}"], writes=[pfx + f"aa{f2}"] + trk)
                    P.op("act", (lambda a2: lambda e: e.activation(out=a2, in_=a2, func=AF.Sqrt, scale=-1.0, bias=one_t[:, :]))(a2), reads=[pfx + f"a2{f2}", "one_t"], writes=[pfx + f"a2{f2}"])
                for f2 in range(2):
                    ft = 2 * nb + f2
                    tr, ti, a2, hh = trb[f2], tib[f2], a2b[f2], hb[f2]
                    trk = [pfx + f"tr{f2}_{cs}" for (cs, n) in chunks_m]
                    tik = [pfx + f"ti{f2}_{cs}" for (cs, n) in chunks_m]
                    if ps == 0:
                        P.op("dve", (lambda a2: lambda e: e.memset(a2[:, 0:1], 1.0))(a2), reads=[], writes=[pfx + f"a2{f2}"])
                    P.op("dve", (lambda ti, f2: lambda e: e.scalar_tensor_tensor(out=ti, in0=ti, scalar=1.0, in1=xc[:, f2, :], op0=ALU.add, op1=ALU.mult))(ti, f2), reads=tik + [pfx + f"xc{f2}"], writes=tik + [pfx + f"uu{f2}"])
                    P.op("dve", (lambda ti, a2: lambda e: e.scalar_tensor_tensor(out=ti, in0=ti, scalar=0.5, in1=a2, op0=ALU.mult, op1=ALU.mult))(ti, a2), reads=[pfx + f"uu{f2}", pfx + f"a2{f2}"], writes=[pfx + f"uu{f2}"] + tik)
                    P.op("dve", (lambda hh, tr, ti, ft: lambda e: e.tensor_tensor_scan(out=hh, data0=tr, data1=ti, initial=hlast[:, ft:ft + 1], op0=ALU.mult, op1=ALU.add))(hh, tr, ti, ft), reads=[pfx + f"aa{f2}", pfx + f"uu{f2}", "hlast"] + trk + tik, writes=[pfx + f"h{f2}"])
                    P.op("dve", (lambda hh, ft: lambda e: e.tensor_copy(out=hlast[:, ft:ft + 1], in_=hh[:, W - 1:W]))(hh, ft), reads=[pfx + f"h{f2}"], writes=["hlast"])

            for f2 in range(2):
                ftp = 14 + f2
                z_tile(load_win(OFF_GR + 128 * ftp), rhs_h, rhs_h_keys, z_chunks(ps, False), gr_epi(ftp))

            P.barrier()
            zpool[0] = [(Zb[0], "Z0"), (Zb[1], "Z1"), (Zb[2], "Z2")] + [(OB, "OBz"), (DB, "DBz"), (Sb[0], "S0"), (Sb[1], "S1"), (Sb[2], "S2")]
            RW.off = RW_MARK
            MX = RW.take((16, NT), BF16)
            tA = [RW.take((NT,), F32)] * 2
            tB = [RW.take((NT,), F32)] * 2
            m1 = [RW.take((512,), F32) for _ in range(2)]
            m2 = [RW.take((512,), F32) for _ in range(2)]

            def mg_epi(dst, dkey, ct):
                def epi(zb, zk, cs, n):
                    t0 = cs - OFF
                    P.op("act", lambda e: e.activation(out=dst[:, t0:t0 + n], in_=zb[:, 0:n], func=AF.Tanh, scale=0.5, bias=hbin_c[:, ct:ct + 1]), reads=[zk, "hbin_c"], writes=[dkey + f"_{t0}"])
                return epi

            def load_w(view, c0):
                def f(w, wk):
                    P.dma("pool", wk, lambda e: e.dma_start(out=w[:], in_=view[:, :, c0:c0 + 128]), writes=[wk])
                return f

            def y_epi(tsrc, tkey, mdst, mkey_fn, final_c=None, mother=None):
                def epi(zb, zk, t0, n):
                    i = (t0 // 512) % 2
                    P.op("dve", lambda e: e.scalar_tensor_tensor(out=mdst[i][:, 0:n], in0=tsrc[:, t0:t0 + n], scalar=1.0, in1=zb[:, 0:n], op0=ALU.add, op1=ALU.mult), reads=[zk, tkey + f"_{t0}"], writes=[mkey_fn(i)])
                    if final_c is not None:
                        P.op("dve", lambda e: e.tensor_tensor(out=MX[:, final_c, t0:t0 + n], in0=mother[i][:, 0:n], in1=mdst[i][:, 0:n], op=ALU.add), reads=[pfx + f"m1_{i}", pfx + f"m2_{i}"], writes=[pfx + f"MX{final_c}_{t0}"])
                return epi

            ychunks = [(0, 512), (512, 512)]
            for c in range(16):
                i = c % 2
                z_tile(load_win(OFF_G + 128 * c), rhs_h, rhs_h_keys, z_chunks(ps, False), mg_epi(tA[i], pfx + "tA", OFF_G // 128 + c))
                z_tile(load_win(OFF_G + D + 128 * c), rhs_h, rhs_h_keys, z_chunks(ps, False), mg_epi(tB[i], pfx + "tB", (OFF_G + D) // 128 + c))
                z_tile(load_w(w_rnn_v, 128 * c), lambda kc, cs, n: AT[:, kc, cs:cs + n], lambda kc, cs, n: [pfx + f"AT{kc}_{cs}"], ychunks,
                       y_epi(tA[i], pfx + "tA", m1, lambda ii: pfx + f"m1_{ii}"))
                z_tile(load_w(w_att_v, 128 * c), lambda kc, cs, n: OG[:, kc, cs:cs + n], lambda kc, cs, n: [pfx + f"OG{kc}_{cs}"], ychunks,
                       y_epi(tB[i], pfx + "tB", m2, lambda ii: pfx + f"m2_{ii}", final_c=c, mother=m1))

            P.barrier()
            if DEBUG:
                P.dma("sp", "dbg1", lambda e: e.dma_start(out=dbg["dbg_hT"][ps], in_=hT), writes=["dbg1"])
                P.dma("sp", "dbg2", lambda e: e.dma_start(out=dbg["dbg_AT"][ps], in_=AT), writes=["dbg2"])
                P.dma("sp", "dbg3", lambda e: e.dma_start(out=dbg["dbg_OG"][ps], in_=OG), writes=["dbg3"])
                P.dma("sp", "dbg4", lambda e: e.dma_start(out=dbg["dbg_MX"][ps], in_=MX), writes=["dbg4"])
                P.dma("sp", "dbg5", lambda e: e.dma_start(out=dbg["dbg_KT"][ps], in_=KT), writes=["dbg5"])
                P.dma("sp", "dbg6", lambda e: e.dma_start(out=dbg["dbg_Vt"][ps], in_=Vt), writes=["dbg6"])
                P.barrier()
            RH.reset()
            Gp = RH.take((D,), F32)
            Bp = RH.take((D,), F32)
            LG = RH.take((D,), F32)
            LB = RH.take((D,), F32)
            RAO.reset()
            wo = RAO.take((4, 16, 512), BF16)
            RW.off = RW_MARK
            _mx = RW.take((16, NT), BF16)
            xf0 = RW.take((D,), F32)
            yf0 = RW.take((D,), F32)
            xfb = [xf0, wv[:].rearrange("p a b -> p (a b)").bitcast(F32)]
            yfb = [yf0, wz_all[:, 0:2].rearrange("p a b c -> p (a b c)").bitcast(F32)]
            xf, yf = xf0, yf0
            nmr = sinkst[:, 0:1]
            for cg in range(4):
                P.dma("pool", pfx + f"wo{cg}", (lambda cg: lambda e: e.dma_start(out=wo[:, cg, :, :], in_=w_o_v[:, :, cg * 512:(cg + 1) * 512]))(cg), writes=[pfx + f"wo{cg}"])
            P.dma("sp", dsem(), lambda e: e.dma_start(out=Gp, in_=gebc_d), writes=[pfx + "Gp"])
            P.dma("sp", dsem(), lambda e: e.dma_start(out=Bp, in_=bebc_d), writes=[pfx + "Bp"])
            P.dma("sp", dsem(), lambda e: e.dma_start(out=yf0, in_=bobc_d), writes=[pfx + "yf0"])
            P.dma("sp", dsem(), lambda e: e.dma_start(out=LG, in_=lgbc_d), writes=[pfx + "LG"])
            P.dma("sp", dsem(), lambda e: e.dma_start(out=LB, in_=lbbc_d), writes=[pfx + "LB"])
            P.op("dve", lambda e: e.tensor_scalar(out=Gp, in0=Gp, scalar1=ALPHA, scalar2=None, op0=ALU.mult), reads=[pfx + "Gp"], writes=[pfx + "Gp"])
            P.op("dve", lambda e: e.scalar_tensor_tensor(out=Bp, in0=Bp, scalar=ALPHA, in1=yf0, op0=ALU.mult, op1=ALU.add), reads=[pfx + "Bp", pfx + "yf0"], writes=[pfx + "Bp"])
            banks = [(Zb[0], "Z0"), (Zb[1], "Z1"), (Zb[2], "Z2"), (Sb[0], "S0"), (Sb[1], "S1"), (Sb[2], "S2"), (OB, "OBf"), (DB, "DBf")]
            def load_x(tb):
                r0 = ps * NT + tb * 128
                xb_ = xfb[tb % 2]
                P.dma("sp", dsem(), (lambda r0, xb_: lambda e: e.dma_start(out=xb_, in_=x_d[r0:r0 + 128, :]))(r0, xb_), writes=[pfx + f"xf{tb % 2}"])

            def out_block(tb, xf, yf, xk_, yk_):
                r0 = ps * NT + tb * 128
                if tb == 0:
                    load_x(0)
                if tb + 1 < 8:
                    load_x(tb + 1)
                P.op("act", lambda e: e.activation(out=xf, in_=xf, func=AF.Copy, accum_out=ssum[:, 0:1]), reads=[xk_], writes=[xk_, "ssum0"])
                P.op("act", lambda e: e.activation(out=yf, in_=xf, func=AF.Square, accum_out=ssum[:, 1:2]), reads=[xk_], writes=[yk_, "ssum1"] + [yk_ + f"_{cg}" for cg in range(4)])
                P.op("dve", lambda e: e.tensor_scalar(out=mv[:, 0:1], in0=ssum[:, 0:1], scalar1=1.0 / D, scalar2=None, op0=ALU.mult), reads=["ssum0"], writes=["mv"])
                P.op("dve", lambda e: e.tensor_tensor(out=ssum[:, 2:3], in0=mv[:, 0:1], in1=mv[:, 0:1], op=ALU.mult), reads=["mv"], writes=["ssum2"])
                P.op("dve", lambda e: e.scalar_tensor_tensor(out=mv[:, 1:2], in0=ssum[:, 1:2], scalar=1.0 / D, in1=ssum[:, 2:3], op0=ALU.mult, op1=ALU.subtract), reads=["ssum1", "ssum2", "mv"], writes=["mv"])
                P.op("act", lambda e: e.activation(out=rstd[:, :], in_=mv[:, 1:2], func=AF.Sqrt, bias=eps_t[:, :]), reads=["mv", "eps_t"], writes=["rstd"])
                P.op("dve", lambda e: e.reciprocal(out=rstd[:, :], in_=rstd[:, :]), reads=["rstd"], writes=["rstd"])
                P.op("dve", lambda e: e.scalar_tensor_tensor(out=xf, in0=xf, scalar=mv[:, 0:1], in1=Gp, op0=ALU.subtract, op1=ALU.mult), reads=[xk_, "mv", pfx + "Gp"], writes=[xk_])
                P.op("dve", lambda e: e.scalar_tensor_tensor(out=xf, in0=xf, scalar=rstd[:, 0:1], in1=Bp, op0=ALU.mult, op1=ALU.add), reads=[xk_, "rstd", pfx + "Bp"], writes=[xk_])
                bks = []
                for cg in range(4):
                    zb, zk = banks[(tb * 4 + cg) % 8]
                    bks.append((zb, zk))
                    for kc in range(16):
                        P.op("pe", (lambda zb, kc, tb, cg: lambda e: e.matmul(zb[:, :], lhsT=MX[:, kc, tb * 128:(tb + 1) * 128], rhs=wo[:, cg, kc, :], start=(kc == 0), stop=(kc == 15)))(zb, kc, tb, cg), reads=[pfx + f"wo{cg}", pfx + f"MX{kc}_{(tb // 4) * 512}"], writes=[zk], signal=(kc == 15))
                for cg in range(4):
                    zb, zk = bks[cg]
                    P.op("dve", (lambda zb, cg: lambda e: e.scalar_tensor_tensor(out=yf[:, cg * 512:(cg + 1) * 512], in0=zb[:, :], scalar=0.25, in1=xf[:, cg * 512:(cg + 1) * 512], op0=ALU.mult, op1=ALU.add))(zb, cg), reads=[zk, xk_], writes=[yk_ + f"_{cg}"] + ([yk_] if cg == 0 else []))
                yk = [yk_ + f"_{cg}" for cg in range(4)]
                P.op("act", lambda e: e.activation(out=yf, in_=yf, func=AF.Copy, accum_out=ssum[:, 0:1]), reads=[yk_ + f"_{cg}" for cg in range(4)], writes=[yk_, "ssum0"] + [yk_ + f"_{cg}" for cg in range(4)])
                P.op("act", lambda e: e.activation(out=xf, in_=yf, func=AF.Square, accum_out=ssum[:, 1:2]), reads=[yk_], writes=[xk_, "ssum1"])
                P.op("dve", lambda e: e.tensor_scalar(out=mv[:, 0:1], in0=ssum[:, 0:1], scalar1=1.0 / D, scalar2=None, op0=ALU.mult), reads=["ssum0"], writes=["mv"])
                P.op("dve", lambda e: e.tensor_tensor(out=ssum[:, 2:3], in0=mv[:, 0:1], in1=mv[:, 0:1], op=ALU.mult), reads=["mv"], writes=["ssum2"])
                P.op("dve", lambda e: e.scalar_tensor_tensor(out=mv[:, 1:2], in0=ssum[:, 1:2], scalar=1.0 / D, in1=ssum[:, 2:3], op0=ALU.mult, op1=ALU.subtract), reads=["ssum1", "ssum2", "mv"], writes=["mv"])
                P.op("act", lambda e: e.activation(out=rstd[:, :], in_=mv[:, 1:2], func=AF.Sqrt, bias=eps_t[:, :]), reads=["mv", "eps_t"], writes=["rstd"])
                P.op("dve", lambda e: e.reciprocal(out=rstd[:, :], in_=rstd[:, :]), reads=["rstd"], writes=["rstd"])
                P.op("dve", lambda e: e.scalar_tensor_tensor(out=yf, in0=yf, scalar=mv[:, 0:1], in1=LG, op0=ALU.subtract, op1=ALU.mult), reads=yk + ["mv", pfx + "LG"], writes=[yk_] + yk)
                P.op("dve", lambda e: e.scalar_tensor_tensor(out=yf, in0=yf, scalar=rstd[:, 0:1], in1=LB, op0=ALU.mult, op1=ALU.add), reads=[yk_, "rstd", pfx + "LB"], writes=[yk_] + yk)
                P.dma("sp", dsem(), (lambda r0: lambda e: e.dma_start(out=out_d[r0:r0 + 128, :], in_=yf))(r0), reads=[yk_] + yk, writes=[f"outd{r0}"])
            for tb in range(8):
                out_block(tb, xfb[tb % 2], yfb[tb % 2], pfx + f"xf{tb % 2}", pfx + f"yf{tb % 2}")
        for ps in range(2):
            do_pass(ps)
        P.barrier()
        P.emit()
    return nc


def _host_layout(inp):
    f = np.float32
    x = np.ascontiguousarray(inp["x"], dtype=f)
    cols = lambda v: np.ascontiguousarray(np.asarray(v, f).reshape(-1, 128).T)
    bc = lambda v: np.ascontiguousarray(np.broadcast_to(np.asarray(v, f).reshape(1, -1), (128, np.asarray(v).size)))
    b_in = np.asarray(inp["b_in"], f)[0]
    bk = np.stack([np.tile(b_in[OFF_K + 64 * kh:OFF_K + 64 * kh + 64], 2) for kh in range(4)], axis=1)
    conv_w = np.asarray(inp["conv_w"], f)[0]
    convw = np.ascontiguousarray(conv_w.reshape(4, 16, 128).transpose(2, 1, 0).reshape(128, 64))
    sinks = np.asarray(inp["sinks"], f)[0]
    sink_rows = np.zeros((8, 512), f)
    for kh in range(4):
        for half in range(2):
            for i in range(4):
                sink_rows[kh * 2 + half, i * 128:(i + 1) * 128] = sinks[8 * kh + 2 * i + half]
    ident = np.eye(128, dtype=f)
    rotm = np.zeros((128, 128), f)
    for m in range(128):
        d = m % 64
        if d < 32:
            rotm[m + 32, m] = -1.0
        else:
            rotm[m - 32, m] = 1.0
    s = np.arange(128)[:, None]
    q = np.arange(128)[None, :]
    mask_c = (s <= q).astype(ml_dtypes.bfloat16)
    mask_p = (s > q).astype(ml_dtypes.bfloat16)
    half = 32
    inv = (10000.0 ** (-np.arange(half, dtype=f) / half)).astype(f)
    pos = np.arange(NMETA + SEQ, dtype=f)
    ang = (pos[None, :] * inv[:, None]).astype(f)
    cos_t = np.ascontiguousarray(np.tile(np.cos(ang).astype(f), (4, 1)))
    sin_t = np.ascontiguousarray(np.tile(np.sin(ang).astype(f), (4, 1)))
    common = {
        "meta": np.ascontiguousarray(inp["meta_tokens"], dtype=f),
        "w_in": np.ascontiguousarray(inp["w_in"][0], dtype=f),
        "w_ra": np.ascontiguousarray(inp["w_ra"][0], dtype=f),
        "w_ri": np.ascontiguousarray(inp["w_ri"][0], dtype=f),
        "w_rnn_out": np.ascontiguousarray(inp["w_rnn_out"][0], dtype=f),
        "w_attn_out": np.ascontiguousarray(inp["w_attn_out"][0], dtype=f),
        "w_o": np.ascontiguousarray(inp["w_o"][0], dtype=f),
        "bin_cols": cols(b_in),
        "bk_dup": np.ascontiguousarray(bk),
        "convw": convw,
        "convb": cols(inp["conv_b"][0]),
        "bra": cols(inp["b_ra"][0]),
        "bri": cols(inp["b_ri"][0]),
        "lam": cols(inp["lru_lambda"][0]),
        "ge_cols": cols(inp["ln_emb_g"]),
        "be_cols": cols(inp["ln_emb_b"]),
        "bv_bc": bc(b_in[OFF_V:OFF_V + 256]),
        "sink_rows": sink_rows,
        "ge_bc": bc(inp["ln_emb_g"]),
        "be_bc": bc(inp["ln_emb_b"]),
        "bo_bc": bc(inp["b_o"][0]),
        "lg_bc": bc(inp["ln_g"][0]),
        "lb_bc": bc(inp["ln_b"][0]),
        "ident": ident,
        "rotm": rotm,
        "mask_c": mask_c,
        "mask_p": mask_p,
        "cos_t": cos_t,
        "sin_t": sin_t,
    }
    maps = []
    for b in range(8):
        m = dict(common)
        m["x"] = np.ascontiguousarray(x[b])
        maps.append(m)
    return maps


def kernel(**inputs):
    nc = build_program()
    maps = _host_layout(inputs)
    res = run_bass_kernel_spmd(nc, maps, core_ids=list(range(8)))
    out = np.stack([np.asarray(r["out"], dtype=np.float32) for r in res.results], axis=0)
    return out
```

```python
import contextlib
import numpy as np
import ml_dtypes
import concourse.bass as bass
import concourse.mybir as mybir
from concourse.bass_utils import run_bass_kernel_spmd

F32 = mybir.dt.float32
BF16 = mybir.dt.bfloat16
AF = mybir.ActivationFunctionType
ALU = mybir.AluOpType

D = 2048
SEQ = 2048
NMETA = 16
NT = 1024
OFF_GR, OFF_Q, OFF_K, OFF_V, OFF_GA, OFF_G = 2048, 4096, 6144, 6400, 6656, 8704
D_IN = 12800
LN_EPS = 1e-5
ALPHA = 2.0 ** 0.25
ENGS = ("pe", "act", "dve", "pool", "sp")
DEBUG = False


class Prog:
    def __init__(self, nc):
        self.nc = nc
        self.q = {e: [] for e in ENGS}
        self.cnt = {e: 0 for e in ENGS}
        self.seen = {e: {} for e in ENGS}
        self.lastw = {}
        self.readers = {}
        self.dma_cnt = {}
        self.sem_names = set(ENGS)

    def _need(self, eng, tok, waits):
        if tok is None:
            return
        s, v = tok
        if s == eng and eng == "pe":
            return
        if self.seen[eng].get(s, 0) >= v:
            return
        waits[s] = max(waits.get(s, 0), v)

    def _emit_waits(self, eng, waits):
        for s, v in waits.items():
            self.seen[eng][s] = v
            self.q[eng].append(("wait", s, v))

    def _deps(self, eng, reads, writes):
        waits = {}
        for k in reads:
            self._need(eng, self.lastw.get(k), waits)
        for k in writes:
            self._need(eng, self.lastw.get(k), waits)
            for t in self.readers.get(k, ()):
                self._need(eng, t, waits)
        self._emit_waits(eng, waits)

    def _commit(self, tok, reads, writes):
        for k in reads:
            self.readers.setdefault(k, []).append(tok)
        for k in writes:
            self.lastw[k] = tok
            self.readers[k] = []

    def op(self, eng, fn, reads=(), writes=(), signal=True):
        reads = tuple(reads)
        writes = tuple(writes)
        self._deps(eng, reads, writes)
        tok = (eng, self.cnt[eng] + 1)
        self._commit(tok, reads, writes)
        if signal:
            self.cnt[eng] += 1
            self.q[eng].append(("op", fn, eng, 1))
        else:
            self.q[eng].append(("op", fn, None, 0))

    def dma(self, queue, sem, fn, reads=(), writes=()):
        reads = tuple(reads)
        writes = tuple(writes)
        self.sem_names.add(sem)
        self._deps(queue, reads, writes)
        if self.dma_cnt.get(sem, 0) > 0:
            w = {}
            self._need(queue, (sem, self.dma_cnt[sem]), w)
            self._emit_waits(queue, w)
        self.dma_cnt[sem] = self.dma_cnt.get(sem, 0) + 16
        tok = (sem, self.dma_cnt[sem])
        self._commit(tok, reads, writes)
        self.q[queue].append(("op", fn, sem, 16))

    def barrier(self):
        for e in ENGS:
            waits = {}
            for s in ENGS:
                if s != e:
                    self._need(e, (s, self.cnt[s]), waits)
            for s, v in self.dma_cnt.items():
                self._need(e, (s, v), waits)
            self._emit_waits(e, waits)

    def emit(self):
        nc = self.nc
        with contextlib.ExitStack() as st:
            sems = {}
            for s in sorted(self.sem_names):
                sems[s] = st.enter_context(nc.semaphore("sem_" + s))
            block = st.enter_context(nc.Block())

            def replay(e, name):
                for item in self.q[name]:
                    if item[0] == "wait":
                        e.wait_ge(sems[item[1]], item[2])
                    else:
                        _, fn, s, inc = item
                        ins = fn(e)
                        if s is not None:
                            ins.then_inc(sems[s], inc)

            @block.tensor
            def _(e):
                replay(e, "pe")

            @block.scalar
            def _(e):
                replay(e, "act")

            @block.vector
            def _(e):
                replay(e, "dve")

            @block.gpsimd
            def _(e):
                replay(e, "pool")

            @block.sync
            def _(e):
                replay(e, "sp")


class Region:
    def __init__(self, ap_f32, nwords):
        self.t = ap_f32
        self.n = nwords
        self.off = 0

    def reset(self):
        self.off = 0

    def take(self, shape, dt):
        n = int(np.prod(shape))
        esz = 2 if dt == BF16 else 4
        words = (n * esz + 3) // 4
        words = (words + 7) // 8 * 8
        assert self.off + words <= self.n, ("region overflow", self.off, words, self.n)
        v = self.t[:, self.off:self.off + words]
        self.off += words
        if dt == BF16:
            v = v.bitcast(BF16)
        v = v[:, 0:n]
        if len(shape) == 2:
            v = v.rearrange("p (a b) -> p a b", a=shape[0])
        elif len(shape) == 3:
            v = v.rearrange("p (a b c) -> p a b c", a=shape[0], b=shape[1])
        return v


def build_program():
    nc = bass.Bass("TRN2", target_bir_lowering=False)
    P = Prog(nc)

    def din(name, shape, dt=F32):
        return nc.dram_tensor(name, list(shape), dt, kind="ExternalInput").ap()

    x_d = din("x", [SEQ, D])
    meta_d = din("meta", [NMETA, D])
    w_in_d = din("w_in", [D, D_IN])
    w_ra_d = din("w_ra", [8, 256, 256])
    w_ri_d = din("w_ri", [8, 256, 256])
    w_rnn_d = din("w_rnn_out", [D, D])
    w_att_d = din("w_attn_out", [D, D])
    w_o_d = din("w_o", [D, D])
    bin_d = din("bin_cols", [128, 100])
    bk_d = din("bk_dup", [128, 4])
    convw_d = din("convw", [128, 64])
    convb_d = din("convb", [128, 16])
    bra_d = din("bra", [128, 16])
    bri_d = din("bri", [128, 16])
    lam_d = din("lam", [128, 16])
    ge_d = din("ge_cols", [128, 16])
    be_d = din("be_cols", [128, 16])
    bv_d = din("bv_bc", [128, 256])
    sink_d = din("sink_rows", [8, 512])
    gebc_d = din("ge_bc", [128, D])
    bebc_d = din("be_bc", [128, D])
    bobc_d = din("bo_bc", [128, D])
    lgbc_d = din("lg_bc", [128, D])
    lbbc_d = din("lb_bc", [128, D])
    ident_d = din("ident", [128, 128])
    rm_d = din("rotm", [128, 128])
    mc_d = din("mask_c", [128, 128], BF16)
    mp_d = din("mask_p", [128, 128], BF16)
    cos_d = din("cos_t", [128, NMETA + SEQ])
    sin_d = din("sin_t", [128, NMETA + SEQ])
    out_d = nc.dram_tensor("out", [SEQ, D], F32, kind="ExternalOutput").ap()

    dbg = {}
    if DEBUG:
        for nm, shp, dt in (("dbg_hT", [2, 128, 16, NMETA + NT], BF16), ("dbg_AT", [2, 128, 16, NT], BF16), ("dbg_OG", [2, 128, 16, NT], BF16),
                            ("dbg_MX", [2, 128, 16, NT], BF16), ("dbg_out", [2, 128, 8, D], F32), ("dbg_KT", [2, 128, 4, NMETA + 128 + NT], BF16),
                            ("dbg_Vt", [2, 128, 9, 4, 64], BF16)):
            dbg[nm] = nc.dram_tensor(nm, shp, dt, kind="ExternalOutput").ap()
    w_in_v = w_in_d.rearrange("(kc p) c -> p kc c", p=128)
    w_rnn_v = w_rnn_d.rearrange("(kc p) c -> p kc c", p=128)
    w_att_v = w_att_d.rearrange("(kc p) c -> p kc c", p=128)
    w_o_v = w_o_d.rearrange("(kc p) c -> p kc c", p=128)

    st = contextlib.ExitStack()
    with st:
        def sb(name, shape, dt):
            return st.enter_context(nc.sbuf_tensor("s_" + name, list(shape), dt))

        def psb(name):
            return st.enter_context(nc.psum_tensor("ps_" + name, [128, 512], F32))

        ident = sb("ident", [128, 128], F32)
        rotm = sb("rotm", [128, 128], F32)
        mask_c = sb("mask_c", [128, 128], BF16)
        mask_p = sb("mask_p", [128, 128], BF16)
        ones = sb("ones", [128, 64], BF16)
        bin_c = sb("bin_c", [128, 100], F32)
        hbin_c = sb("hbin_c", [128, 100], F32)
        bk_c = sb("bk_c", [128, 4], F32)
        convw = sb("convw", [128, 64], F32)
        convb = sb("convb", [128, 16], F32)
        hbra = sb("hbra", [128, 16], F32)
        hbri = sb("hbri", [128, 16], F32)
        lam = sb("lam", [128, 16], F32)
        cl = sb("cl", [128, 16], F32)
        hcl = sb("hcl", [128, 16], F32)
        ge_c = sb("ge_c", [128, 16], F32)
        be_c = sb("be_c", [128, 16], F32)
        bv_bc = sb("bv_bc", [128, 256], F32)
        eps_t = sb("eps_t", [128, 1], F32)
        one_t = sb("one_t", [128, 1], F32)
        halo = sb("halo", [128, 16, 3], F32)
        hlast = sb("hlast", [128, 16], F32)
        stats = sb("stats", [128, 4, 6], F32)
        mv = sb("mv", [128, 2], F32)
        rstd = sb("rstd", [128, 1], F32)
        stats_b = sb("stats_b", [128, 4, 6], F32)
        ssum = sb("ssum", [128, 4], F32)
        mv_b = sb("mv_b", [128, 2], F32)
        rstd_b = sb("rstd_b", [128, 1], F32)
        sinkst = sb("sinkst", [128, 512], F32)
        ptm = [sb(f"ptm{v}", [128, 512], BF16) for v in range(8)]
        vm = sb("vm", [128, 4, 64], BF16)
        wz_all = sb("wz_all", [128, 3, 16, 128], BF16)
        wz = [wz_all[:, i] for i in range(3)]
        wv = sb("wv", [128, 16, 256], BF16)
        wg = [sb(f"wg{i}", [128, 2, 2, 256], BF16) for i in range(2)]
        RH_W = 8320
        RAO_W = 16384
        RW_W = 16384
        rh_t = sb("rh", [128, RH_W], F32)
        rao_t = sb("rao", [128, RAO_W], F32)
        rw_t = sb("rw", [128, RW_W], F32)
        RH = Region(rh_t[:], RH_W)
        RAO = Region(rao_t[:], RAO_W)
        RW = Region(rw_t[:], RW_W)

        Zb = [psb(f"zb{i}") for i in range(3)]
        Sb = [psb(f"sbk{i}") for i in range(3)]
        OB = psb("ob")
        DB = psb("db")
        zctr = [0]

        zpool = [[(Zb[0], "Z0"), (Zb[1], "Z1"), (Zb[2], "Z2")]]

        def next_z():
            pool = zpool[0]
            i = zctr[0] % len(pool)
            zctr[0] += 1
            return pool[i]

        sctr = [0]

        def next_s():
            i = sctr[0] % 3
            sctr[0] += 1
            return Sb[i], f"S{i}"

        dctr = [0]

        def dsem():
            dctr[0] += 1
            return f"d{dctr[0] % 6}"

        def ld(dst, src, key, q="sp"):
            P.dma(q, "c" + key, lambda e: e.dma_start(out=dst, in_=src), writes=[key])

        ld(ident[:], ident_d, "ident")
        ld(ge_c[:], ge_d, "ge_c")
        ld(be_c[:], be_d, "be_c")
        P.op("dve", lambda e: e.memset(eps_t[:], LN_EPS), writes=["eps_t"])

        def late_setup():
            ld(rotm[:], rm_d, "rotm")
            ld(mask_c[:], mc_d, "mask_c")
            ld(mask_p[:], mp_d, "mask_p")
            ld(bin_c[:], bin_d, "bin_c")
            ld(bk_c[:], bk_d, "bk_c")
            ld(convw[:], convw_d, "convw")
            ld(convb[:], convb_d, "convb")
            ld(hbra[:], bra_d, "hbra")
            ld(hbri[:], bri_d, "hbri")
            ld(lam[:], lam_d, "lam")
            ld(bv_bc[:], bv_d, "bv_bc")
            P.op("dve", lambda e: e.memset(ones[:], 1.0), writes=["ones"])
            P.op("dve", lambda e: e.memset(one_t[:], 1.0), writes=["one_t"])
            P.op("dve", lambda e: e.memset(halo[:], 0.0), writes=["halo"])
            P.op("dve", lambda e: e.memset(hlast[:], 0.0), writes=["hlast"])
            P.op("dve", lambda e: e.memset(vm[:], 0.0), writes=["vm"])
            P.op("dve", lambda e: e.tensor_scalar(out=hbin_c[:], in0=bin_c[:], scalar1=0.5, scalar2=None, op0=ALU.mult), reads=["bin_c"], writes=["hbin_c"])
            P.op("dve", lambda e: e.tensor_scalar(out=hbra[:], in0=hbra[:], scalar1=0.5, scalar2=None, op0=ALU.mult), reads=["hbra"], writes=["hbra"])
            P.op("dve", lambda e: e.tensor_scalar(out=hbri[:], in0=hbri[:], scalar1=0.5, scalar2=None, op0=ALU.mult), reads=["hbri"], writes=["hbri"])
            P.op("act", lambda e: e.activation(out=cl[:], in_=lam[:], func=AF.Exp, scale=-1.0), reads=["lam"], writes=["cl"])
            P.op("dve", lambda e: e.tensor_scalar(out=cl[:], in0=cl[:], scalar1=1.0, scalar2=None, op0=ALU.add), reads=["cl"], writes=["cl"])
            P.op("act", lambda e: e.activation(out=cl[:], in_=cl[:], func=AF.Ln), reads=["cl"], writes=["cl"])
            P.op("dve", lambda e: e.tensor_scalar(out=hcl[:], in0=cl[:], scalar1=-4.0, scalar2=None, op0=ALU.mult), reads=["cl"], writes=["hcl"])
            P.op("dve", lambda e: e.tensor_scalar(out=cl[:], in0=cl[:], scalar1=-8.0, scalar2=None, op0=ALU.mult), reads=["cl", "hcl"], writes=["cl"])

        hT = RH.take((16, NMETA + NT), BF16)
        AT = RAO.take((16, NT), BF16)
        OG = RAO.take((16, NT), BF16)
        KT = RW.take((4, NMETA + 128 + NT), BF16)
        Vt = RW.take((9, 4, 64), BF16)
        RW_MARK = RW.off

        wctr = [0]

        def load_win(c0):
            def f(w, wk):
                P.dma("pool", wk, lambda e: e.dma_start(out=w[:], in_=w_in_v[:, :, c0:c0 + 128]), writes=[wk])
            return f

        def load_kdup(kh):
            def f(w, wk):
                c0 = OFF_K + 64 * kh
                P.dma("pool", wk, lambda e: e.dma_start(out=w[:, :, 0:64], in_=w_in_v[:, :, c0:c0 + 64]), writes=[wk])
                P.dma("pool", wk + "b", lambda e: e.dma_start(out=w[:, :, 64:128], in_=w_in_v[:, :, c0:c0 + 64]), writes=[wk])
            return f

        preq = []

        def preload(load_fn):
            i = wctr[0] % 3
            wctr[0] += 1
            load_fn(wz[i], f"wz{i}")
            preq.append((wz[i], f"wz{i}"))

        def load_wg(nb):
            wgi = wg[nb % 2]
            wgk = f"wg{nb % 2}"
            P.dma("pool", wgk + "r", lambda e: e.dma_start(out=wgi[:, 0, :, :], in_=w_ra_d[nb].rearrange("(kc p) d -> p kc d", p=128)), writes=[wgk + "r"])
            P.dma("pool", wgk + "i", lambda e: e.dma_start(out=wgi[:, 1, :, :], in_=w_ri_d[nb].rearrange("(kc p) d -> p kc d", p=128)), writes=[wgk + "i"])


        def z_chunks(ps, with_meta):
            if ps == 0:
                ch = [(NMETA, 512), (NMETA + 512, 512)]
                if with_meta:
                    ch = [(0, NMETA)] + ch
                return ch
            return [(0, 512), (512, 512)]

        def ht_keys(ps, cs, n):
            if ps == 0:
                if cs == 0:
                    return ["hT_m"]
                b0 = (cs - NMETA) // 128
            else:
                b0 = cs // 128
            return [f"hT_{b0 + i}" for i in range((n + 127) // 128)]

        def do_pass(ps):
            OFF = NMETA if ps == 0 else 0
            W = OFF + NT
            pfx = f"p{ps}_"
            P.barrier()
            P.dma("pool", "wv", lambda e: e.dma_start(out=wv[:], in_=w_in_v[:, :, OFF_V:OFF_V + 256]), writes=["wv"])
            preload(load_kdup(0))
            preload(load_kdup(1))
            zpool[0] = [(Zb[0], "Z0"), (Zb[1], "Z1"), (Zb[2], "Z2")]
            RW.off = RW_MARK
            xbuf = [RW.take((D,), F32) for _ in range(2)]
            xnb = [RW.take((D,), F32) for _ in range(2)]
            blocks = ([("m", meta_d, NMETA, 0)] if ps == 0 else []) + [
                (str(b), x_d[ps * NT + b * 128: ps * NT + (b + 1) * 128, :], 128, OFF + b * 128) for b in range(8)]
            stA, stB = [], []
            lnbanks = [(Zb[0], "Z0"), (Zb[1], "Z1"), (Zb[2], "Z2"), (Sb[0], "S0"), (Sb[1], "S1"), (Sb[2], "S2")]
            lnctr = [0]
            for bi, (bn, src, n, col0) in enumerate(blocks):
                def A_(bi=bi, bn=bn, src=src, n=n, col0=col0):
                    xb = xbuf[bi % 2]
                    xn = xnb[bi % 2]
                    kx = pfx + f"xb{bi % 2}"
                    kn = pfx + f"xn{bi % 2}"
                    st_, mv_, rs_ = (stats, mv, rstd) if bi % 2 == 0 else (stats_b, mv_b, rstd_b)
                    sfx = "" if bi % 2 == 0 else "_b"
                    P.dma("sp", dsem(), (lambda xb, src, n: lambda e: e.dma_start(out=xb[0:n, :], in_=src))(xb, src, n), writes=[kx])
                    for j in range(4):
                        P.op("dve", (lambda xb, n, j, st_: lambda e: e.bn_stats(out=st_[0:n, j, :], in_=xb[0:n, j * 512:(j + 1) * 512]))(xb, n, j, st_), reads=[kx], writes=[f"stats{j}" + sfx])
                    P.op("dve", (lambda n, st_, mv_: lambda e: e.bn_aggr(out=mv_[0:n, :], in_=st_[0:n].rearrange("p a b -> p (a b)")))(n, st_, mv_), reads=[f"stats{j}" + sfx for j in range(4)], writes=["mv" + sfx])
                    P.op("act", (lambda n, mv_, rs_: lambda e: e.activation(out=rs_[0:n, :], in_=mv_[0:n, 1:2], func=AF.Sqrt, bias=eps_t[0:n, :]))(n, mv_, rs_), reads=["mv" + sfx, "eps_t"], writes=["rstd" + sfx])
                    P.op("dve", (lambda n, rs_: lambda e: e.reciprocal(out=rs_[0:n, :], in_=rs_[0:n, :]))(n, rs_), reads=["rstd" + sfx], writes=["rstd" + sfx])
                    P.op("dve", (lambda xb, xn, n, mv_, rs_: lambda e: e.tensor_scalar(out=xn[0:n, :], in0=xb[0:n, :], scalar1=mv_[0:n, 0:1], scalar2=rs_[0:n, 0:1], op0=ALU.subtract, op1=ALU.mult))(xb, xn, n, mv_, rs_), reads=[kx, "mv" + sfx, "rstd" + sfx], writes=[kn])
                def B_(bi=bi, bn=bn, src=src, n=n, col0=col0):
                    xn = xnb[bi % 2]
                    kn = pfx + f"xn{bi % 2}"
                    for g4 in range(4):
                        zb, zk = lnbanks[lnctr[0] % 6]
                        lnctr[0] += 1
                        for j in range(4):
                            kc = g4 * 4 + j
                            P.op("pe", (lambda zb, xn, n, j, kc: lambda e: e.transpose(out=zb[:, j * 128:j * 128 + n], in_=xn[0:n, kc * 128:(kc + 1) * 128], identity=ident[0:n, 0:n]))(zb, xn, n, j, kc), reads=[kn, "ident"], writes=[zk], signal=(j == 3))
                        for j in range(4):
                            kc = g4 * 4 + j
                            if g4 != 3:
                                P.op("act", (lambda zb, n, j, kc, col0: lambda e: e.activation(out=hT[:, kc, col0:col0 + n], in_=zb[:, j * 128:j * 128 + n], func=AF.Identity, scale=ge_c[:, kc:kc + 1], bias=be_c[:, kc:kc + 1]))(zb, n, j, kc, col0), reads=[zk, "ge_c", "be_c"], writes=[f"hT_{bn}_{kc}"])
                            else:
                                P.op("dve", (lambda zb, n, j, kc, col0: lambda e: e.tensor_scalar(out=hT[:, kc, col0:col0 + n], in0=zb[:, j * 128:j * 128 + n], scalar1=ge_c[:, kc:kc + 1], scalar2=be_c[:, kc:kc + 1], op0=ALU.mult, op1=ALU.add))(zb, n, j, kc, col0), reads=[zk, "ge_c", "be_c"], writes=[f"hT_{bn}_{kc}"])
                stA.append(A_)
                stB.append(B_)
            stA[0]()
            for bi in range(len(blocks)):
                if bi + 1 < len(blocks):
                    stA[bi + 1]()
                stB[bi]()
            if ps == 0:
                late_setup()
            def htk(ps, cs, n):
                ks = []
                for b in ht_keys(ps, cs, n):
                    bn = b[3:]
                    ks += [f"hT_{bn}_{kc}" for kc in range(16)]
                return ks

            P.barrier()
            RW.off = RW_MARK
            cosT = RW.take((W,), F32)
            sinT = RW.take((W,), F32)
            p0 = 0 if ps == 0 else NMETA + NT
            P.dma("sp", dsem(), lambda e: e.dma_start(out=cosT, in_=cos_d[:, p0:p0 + W]), writes=[pfx + "cosT"])
            P.dma("sp", dsem(), lambda e: e.dma_start(out=sinT, in_=sin_d[:, p0:p0 + W]), writes=[pfx + "sinT"])
            Qg = [RW.take((4, NT), BF16) for _ in range(2)]
            PT = [[RW.take((512,), BF16) for _ in range(2)] for _ in range(2)]
            kz = [RW.take((512,), F32) for _ in range(2)]
            t1b = [RW.take((512,), F32) for _ in range(2)]
            t2b = [RW.take((512,), F32) for _ in range(2)]
            tgb = [RW.take((512,), F32) for _ in range(1)]
            xgb = [RW.take((512,), F32) for _ in range(1)]
            rden = RW.take((512,), F32)
            ogt = RW.take((512,), F32)

            for bi, (bn, src, n, col0) in enumerate(blocks):
                zb, zk = next_z()
                for kc in range(16):
                    P.op("pe", (lambda zb, n, kc, col0: lambda e: e.matmul(zb[0:n, 0:256], lhsT=hT[:, kc, col0:col0 + n], rhs=wv[:, kc, :], start=(kc == 0), stop=(kc == 15)))(zb, n, kc, col0), reads=["wv", f"hT_{bn}_{kc}"], writes=[zk], signal=(kc == 15))
                if bn == "m":
                    P.op("dve", (lambda zb: lambda e: e.tensor_tensor(out=vm[0:16, :, :], in0=zb[0:16, 0:256].rearrange("p (a b) -> p a b", a=4), in1=bv_bc[0:16, :].rearrange("p (a b) -> p a b", a=4), op=ALU.add))(zb), reads=[zk, "bv_bc", "vm"], writes=["vm"])
                else:
                    slot = int(bn) + 1
                    P.op("dve", (lambda zb, slot: lambda e: e.tensor_tensor(out=Vt[:, slot, :, :], in0=zb[:, 0:256].rearrange("p (a b) -> p a b", a=4), in1=bv_bc[:, :].rearrange("p (a b) -> p a b", a=4), op=ALU.add))(zb, slot), reads=[zk, "bv_bc"], writes=[pfx + f"Vt{slot}"])

            if ps == 0:
                for v in range(8):
                    P.op("dve", (lambda v: lambda e: e.memset(ptm[v][:], 0.0))(v), writes=[f"ptm{v}"])
                    P.dma("sp", "csink", (lambda v: lambda e: e.dma_start(out=sinkst[32:33, :], in_=sink_d[v:v + 1, :]))(v), writes=["sinkst"])
                    P.op("act", (lambda v: lambda e: e.activation(out=ptm[v][32:33, :], in_=sinkst[32:33, :], func=AF.Exp))(v), reads=["sinkst"], writes=[f"ptm{v}"])
            rot_q = []
            att_q = []
            PUMP = [2]

            def pump(k):
                while rot_q:
                    rot_q.pop(0)()
                for _ in range(k):
                    if att_q:
                        att_q.pop(0)()

            def pump_all():
                while rot_q or att_q:
                    pump(4)

            def z_tile(load_fn, rhs_fn, rhs_keys_fn, chunks, epi):
                if preq:
                    w, wk = preq.pop(0)
                else:
                    i = wctr[0] % 3
                    wctr[0] += 1
                    w = wz[i]
                    wk = f"wz{i}"
                    load_fn(w, wk)
                for (cs, n) in chunks:
                    zb, zk = next_z()
                    for kc in range(16):
                        P.op("pe", (lambda zb, n, kc, cs, w: lambda e: e.matmul(zb[:, 0:n], lhsT=w[:, kc, :], rhs=rhs_fn(kc, cs, n), start=(kc == 0), stop=(kc == 15)))(zb, n, kc, cs, w), reads=[wk] + rhs_keys_fn(kc, cs, n), writes=[zk], signal=(kc == 15))
                    pump(PUMP[0])
                    epi(zb, zk, cs, n)

            def rhs_h(kc, cs, n):
                return hT[:, kc, cs:cs + n]

            def rhs_h_keys(kc, cs, n):
                return [f"hT_{b[3:]}_{kc}" for b in ht_keys(ps, cs, n)]

            ectr = [0]

            def rope_epi(bias_ap, dst_fn, dst_key_fn):
                def epi(zb, zk, cs, n):
                    i = ectr[0] % 2
                    ectr[0] += 1
                    kzb, t1, t2 = kz[i], t1b[i], t2b[i]
                    P.op("act", lambda e: e.activation(out=kzb[:, 0:n], in_=zb[:, 0:n], func=AF.Identity, bias=bias_ap), reads=[zk, "bin_c", "bk_c"], writes=[pfx + f"kz{i}"])

                    def part2():
                        sbk, sk = next_s()
                        P.op("pe", lambda e: e.matmul(sbk[:, 0:n], lhsT=rotm[:], rhs=kzb[:, 0:n], start=True, stop=True), reads=[pfx + f"kz{i}", "rotm"], writes=[sk])
                        P.op("dve", lambda e: e.tensor_tensor(out=t1[:, 0:n], in0=kzb[:, 0:n], in1=cosT[:, cs:cs + n], op=ALU.mult), reads=[pfx + f"kz{i}", pfx + "cosT"], writes=[pfx + f"t1{i}"])
                        P.op("dve", lambda e: e.tensor_tensor(out=t2[:, 0:n], in0=sbk[:, 0:n], in1=sinT[:, cs:cs + n], op=ALU.mult), reads=[sk, pfx + "sinT"], writes=[pfx + f"t2{i}"])
                        P.op("dve", lambda e: e.tensor_tensor(out=dst_fn(cs, n), in0=t1[:, 0:n], in1=t2[:, 0:n], op=ALU.add), reads=[pfx + f"t1{i}", pfx + f"t2{i}"], writes=[dst_key_fn(cs, n)])
                    rot_q.append(part2)
                return epi

            def kcol(cs):
                return cs if (ps == 0 and cs < NMETA) else NMETA + 128 + (cs - OFF)

            if ps == 1:
                pass
            for kh in range(4):
                z_tile(load_kdup(kh), rhs_h, rhs_h_keys, z_chunks(ps, True),
                       rope_epi(bk_c[:, kh:kh + 1], (lambda kh: lambda cs, n: KT[:, kh, kcol(cs):kcol(cs) + n])(kh), (lambda kh: lambda cs, n: pfx + f"KT{kh}_{kcol(cs)}")(kh)))

            def kt_keys(kh, c0, n):
                ks = []
                if c0 < NMETA:
                    return [f"p0_KT{kh}_0"]
                if c0 < NMETA + 128:
                    return [f"KTprev{kh}"]
                cc = c0 - (NMETA + 128)
                base = (cc // 512) * 512
                return [pfx + f"KT{kh}_{NMETA + 128 + base}"]

            def attention_items(kh):
                qb = Qg[kh % 2]

                def mkA(n, half):
                    def A():
                        g = 8 * ps + n
                        q0 = n * 128
                        kcur = NMETA + 128 + n * 128
                        kprev = kcur - 128
                        r0 = half * 64
                        v = kh * 2 + half
                        ptc, ptp = PT[half]
                        qkeys = [pfx + f"Qg{kh % 2}_{i}_{(q0 // 512) * 512}" for i in range(4)]
                        s0, k0 = next_s()
                        P.op("pe", lambda e: e.matmul(s0[:, :].rearrange("p (a b) -> p a b", a=4), lhsT=KT[r0:r0 + 64, kh, kcur:kcur + 128], rhs=qb[r0:r0 + 64, :, q0:q0 + 128], start=True, stop=True), reads=qkeys + kt_keys(kh, kcur, 128), writes=[k0])
                        P.op("act", lambda e: e.activation(out=ptc, in_=s0[:, :], func=AF.Exp, scale=0.125), reads=[k0], writes=[pfx + f"ptc{half}"])
                        P.op("dve", lambda e: e.tensor_tensor(out=ptc.rearrange("p (a b) -> p a b", a=4), in0=ptc.rearrange("p (a b) -> p a b", a=4), in1=mask_c[:, :].unsqueeze(1).broadcast_to([128, 4, 128]), op=ALU.mult), reads=[pfx + f"ptc{half}", "mask_c"], writes=[pfx + f"ptc{half}"])
                        if g > 0:
                            s1, k1 = next_s()
                            P.op("pe", lambda e: e.matmul(s1[:, :].rearrange("p (a b) -> p a b", a=4), lhsT=KT[r0:r0 + 64, kh, kprev:kprev + 128], rhs=qb[r0:r0 + 64, :, q0:q0 + 128], start=True, stop=True), reads=qkeys + kt_keys(kh, kprev, 128), writes=[k1])
                            P.op("act", lambda e: e.activation(out=ptp, in_=s1[:, :], func=AF.Exp, scale=0.125), reads=[k1], writes=[pfx + f"ptp{half}"])
                            P.op("dve", lambda e: e.tensor_tensor(out=ptp.rearrange("p (a b) -> p a b", a=4), in0=ptp.rearrange("p (a b) -> p a b", a=4), in1=mask_p[:, :].unsqueeze(1).broadcast_to([128, 4, 128]), op=ALU.mult), reads=[pfx + f"ptp{half}", "mask_p"], writes=[pfx + f"ptp{half}"])
                        s2, k2 = next_s()
                        P.op("pe", lambda e: e.matmul(s2[0:16, :].rearrange("p (a b) -> p a b", a=4), lhsT=KT[r0:r0 + 64, kh, 0:NMETA], rhs=qb[r0:r0 + 64, :, q0:q0 + 128], start=True, stop=True), reads=qkeys + kt_keys(kh, 0, 16), writes=[k2])
                        P.op("act", lambda e: e.activation(out=ptm[v][0:16, :], in_=s2[0:16, :], func=AF.Exp, scale=0.125), reads=[k2], writes=[f"ptm{v}"])
                    return A

                def mkB(n, half):
                    def B():
                        g = 8 * ps + n
                        q0 = n * 128
                        r0 = half * 64
                        v = kh * 2 + half
                        ptc, ptp = PT[half]
                        seq = [(Vt[:, n + 1, kh, :], ones[:, :], ptc, 128, [pfx + f"ptc{half}", vt_key(n + 1)])]
                        if g > 0:
                            seq.append((Vt[:, n, kh, :], ones[:, :], ptp, 128, [pfx + f"ptp{half}", vt_key(n)]))
                        seq.append((vm[0:33, kh, :], ones[0:33, :], ptm[v][0:33, :], 33, [f"ptm{v}", "vm"]))
                        ns = len(seq)
                        for si, (vl, ol, pr, kk, rk) in enumerate(seq):
                            P.op("pe", lambda e, vl=vl, pr=pr, si=si: e.matmul(OB[r0:r0 + 64, :], lhsT=vl, rhs=pr, start=(si == 0), stop=(si == ns - 1)), reads=rk, writes=[f"OB{half}"], signal=(si == ns - 1))
                        for si, (vl, ol, pr, kk, rk) in enumerate(seq):
                            P.op("pe", lambda e, ol=ol, pr=pr, si=si: e.matmul(DB[r0:r0 + 64, :], lhsT=ol, rhs=pr, start=(si == 0), stop=(si == ns - 1)), reads=rk + ["ones"], writes=[f"DB{half}"], signal=(si == ns - 1))
                        if half == 1:
                            P.op("dve", lambda e: e.reciprocal(out=rden, in_=DB[:, :]), reads=["DB0", "DB1"], writes=[pfx + "rden"])
                            P.op("dve", lambda e: e.tensor_tensor(out=ogt, in0=OB[:, :], in1=rden, op=ALU.mult), reads=["OB0", "OB1", pfx + "rden"], writes=[pfx + "ogt"])
                            ogk = [pfx + f"OG{4 * kh + i}_{(q0 // 512) * 512}" for i in range(4)]
                            P.op("dve", lambda e: e.tensor_tensor(out=OG[:, 4 * kh:4 * kh + 4, q0:q0 + 128], in0=ogt.rearrange("p (a b) -> p a b", a=4), in1=OG[:, 4 * kh:4 * kh + 4, q0:q0 + 128], op=ALU.mult), reads=[pfx + "ogt"] + ogk, writes=ogk)
                    return B

                units = [(n, h) for n in range(8) for h in range(2)]
                As = [mkA(n, h) for (n, h) in units]
                Bs = [mkB(n, h) for (n, h) in units]
                items = [As[0], As[1]]
                for i in range(16):
                    items.append(Bs[i])
                    if i + 2 < 16:
                        items.append(As[i + 2])
                return items

            def vt_key(slot):
                if slot == 0:
                    return "Vtprev"
                return pfx + f"Vt{slot}"

            def ga_epi(j):
                def epi(zb, zk, cs, n):
                    i = 0
                    tg, xg = tgb[i], xgb[i]
                    ct = OFF_GA // 128 + j
                    t0 = cs - OFF
                    P.op("act", lambda e: e.activation(out=tg[:, 0:n], in_=zb[:, 0:n], func=AF.Tanh, scale=0.5, bias=hbin_c[:, ct:ct + 1]), reads=[zk, "hbin_c"], writes=[pfx + f"tg{i}"])
                    P.op("act", lambda e: e.activation(out=xg[:, 0:n], in_=zb[:, 0:n], func=AF.Identity, bias=bin_c[:, ct:ct + 1]), reads=[zk, "bin_c"], writes=[pfx + f"xg{i}"])
                    P.op("dve", lambda e: e.scalar_tensor_tensor(out=OG[:, j, t0:t0 + n], in0=tg[:, 0:n], scalar=1.0, in1=xg[:, 0:n], op0=ALU.add, op1=ALU.mult), reads=[pfx + f"tg{i}", pfx + f"xg{i}"], writes=[pfx + f"OG{j}_{t0}"])
                return epi

            for kh in range(4):
                for i4 in range(4):
                    j = 4 * kh + i4
                    z_tile(load_win(OFF_GA + 128 * j), rhs_h, rhs_h_keys, z_chunks(ps, False), ga_epi(j))
                for i4 in range(4):
                    j = 4 * kh + i4
                    ct = OFF_Q // 128 + j
                    z_tile(load_win(OFF_Q + 128 * j), rhs_h, rhs_h_keys, z_chunks(ps, False),
                           rope_epi(bin_c[:, ct:ct + 1], (lambda kh, i4: lambda cs, n: Qg[kh % 2][:, i4, cs - OFF:cs - OFF + n])(kh, i4), (lambda kh, i4: lambda cs, n: pfx + f"Qg{kh % 2}_{i4}_{cs - OFF}")(kh, i4)))
                att_q.extend(attention_items(kh))
            pump_all()
            if ps == 0:
                for kh in range(4):
                    P.op("dve", (lambda kh: lambda e: e.tensor_copy(out=KT[:, kh, NMETA:NMETA + 128], in_=KT[:, kh, NMETA + NT:NMETA + NT + 128]))(kh), reads=[f"p0_KT{kh}_{NMETA + 128 + 512}"], writes=[f"KTprev{kh}"])
                P.op("dve", lambda e: e.tensor_copy(out=Vt[:, 0, :, :], in_=Vt[:, 8, :, :]), reads=["p0_Vt8"], writes=["Vtprev"])

            preload(load_win(0))
            preload(load_win(128))
            load_wg(0)
            P.barrier()
            zpool[0] = [(Zb[0], "Z0"), (Zb[1], "Z1"), (Zb[2], "Z2")] + [(OB, "OBz"), (DB, "DBz")]
            RW.off = RW_MARK
            xr = RW.take((2, 3 + W), F32)
            xc = RW.take((2, W), F32)
            xcb = RW.take((2, W), BF16)
            wv_f32 = wv[:].rearrange("p a b -> p (a b)").bitcast(F32)
            trb = [RW.take((W,), F32), wv_f32[:, 0:W]]
            tib = [RW.take((W,), F32) for _ in range(2)]
            a2b = [RW.take((W,), F32) for _ in range(2)]
            hb = [RW.take((W,), F32) for _ in range(2)]
            tg2 = [KT[:, 2, NMETA + 128:NMETA + 128 + NT].bitcast(F32), KT[:, 0, NMETA + 128:NMETA + 128 + NT].bitcast(F32)]
            xg2 = [KT[:, 3, NMETA + 128:NMETA + 128 + NT].bitcast(F32), KT[:, 1, NMETA + 128:NMETA + 128 + NT].bitcast(F32)]
            grctr = [0]

            def xr_epi(ft):
                f2 = ft % 2
                def epi(zb, zk, cs, n):
                    P.op("act", lambda e: e.activation(out=xr[:, f2, 3 + cs:3 + cs + n], in_=zb[:, 0:n], func=AF.Identity, bias=bin_c[:, ft:ft + 1]), reads=[zk, "bin_c"], writes=[pfx + f"xr{f2}_{cs}"])
                return epi

            def gr_epi(ft):
                f2 = ft % 2
                def epi(zb, zk, cs, n):
                    i = grctr[0] % 2
                    grctr[0] += 1
                    tg, xg = tg2[i], xg2[i]
                    ct = OFF_GR // 128 + ft
                    t0 = cs - OFF
                    P.op("act", lambda e: e.activation(out=tg[:, 0:n], in_=zb[:, 0:n], func=AF.Tanh, scale=0.5, bias=hbin_c[:, ct:ct + 1]), reads=[zk, "hbin_c"], writes=[pfx + f"tgr{i}"])
                    P.op("act", lambda e: e.activation(out=xg[:, 0:n], in_=zb[:, 0:n], func=AF.Identity, bias=bin_c[:, ct:ct + 1]), reads=[zk, "bin_c"], writes=[pfx + f"xgr{i}"])
                    P.op("dve", lambda e: e.scalar_tensor_tensor(out=tg[:, 0:n], in0=tg[:, 0:n], scalar=1.0, in1=xg[:, 0:n], op0=ALU.add, op1=ALU.mult), reads=[pfx + f"tgr{i}", pfx + f"xgr{i}"], writes=[pfx + f"tgr{i}"])
                    P.op("dve", lambda e: e.tensor_tensor(out=AT[:, ft, t0:t0 + n], in0=tg[:, 0:n], in1=hb[f2][:, cs:cs + n], op=ALU.mult), reads=[pfx + f"tgr{i}", pfx + f"h{f2}"], writes=[pfx + f"AT{ft}_{t0}"])
                return epi

            chunks_m = z_chunks(ps, True)
            for nb in range(8):
                wgi = wg[nb % 2]
                wgk = f"wg{nb % 2}"
                if nb + 1 < 8:
                    load_wg(nb + 1)
                for f2 in range(2):
                    ft = 2 * nb + f2
                    P.op("dve", (lambda f2, ft: lambda e: e.tensor_copy(out=xr[:, f2, 0:3], in_=halo[:, ft, :]))(f2, ft), reads=["halo"], writes=[pfx + f"xrh{f2}"])
                    z_tile(load_win(128 * ft), rhs_h, rhs_h_keys, chunks_m, xr_epi(ft))
                for f2 in range(2):
                    ft = 2 * nb + f2
                    xk = [pfx + f"xr{f2}_{cs}" for (cs, n) in chunks_m] + [pfx + f"xrh{f2}"]
                    P.op("dve", (lambda f2, ft: lambda e: e.tensor_scalar(out=xc[:, f2, :], in0=xr[:, f2, 3:3 + W], scalar1=convw[:, 4 * ft:4 * ft + 1], scalar2=convb[:, ft:ft + 1], op0=ALU.mult, op1=ALU.add))(f2, ft), reads=xk + ["convw", "convb"], writes=[pfx + f"xc{f2}"])
                    for k in range(1, 4):
                        P.op("dve", (lambda f2, ft, k: lambda e: e.scalar_tensor_tensor(out=xc[:, f2, :], in0=xr[:, f2, 3 - k:3 - k + W], scalar=convw[:, 4 * ft + k:4 * ft + k + 1], in1=xc[:, f2, :], op0=ALU.mult, op1=ALU.add))(f2, ft, k), reads=xk + [pfx + f"xc{f2}"], writes=[pfx + f"xc{f2}"])
                    P.op("dve", (lambda f2, ft: lambda e: e.tensor_copy(out=halo[:, ft, :], in_=xr[:, f2, W:W + 3]))(f2, ft), reads=xk, writes=["halo"])
                    P.op("dve", (lambda f2: lambda e: e.tensor_copy(out=xcb[:, f2, :], in_=xc[:, f2, :]))(f2), reads=[pfx + f"xc{f2}"], writes=[pfx + f"xcb{f2}"])
                if nb > 0:
                    for f2 in range(2):
                        ftp = 2 * (nb - 1) + f2
                        z_tile(load_win(OFF_GR + 128 * ftp), rhs_h, rhs_h_keys, z_chunks(ps, False), gr_epi(ftp))
                for f2 in range(2):
                    ft = 2 * nb + f2
                    tr, ti = trb[f2], tib[f2]
                    for gi, (tdst, hbias, tkey) in enumerate(((tr, hbra, "tr"), (ti, hbri, "ti"))):
                        for (cs, n) in chunks_m:
                            sbk, sk = next_s()
                            for kc2 in range(2):
                                P.op("pe", (lambda sbk, n, gi, kc2, f2, cs, wgi: lambda e: e.matmul(sbk[:, 0:n], lhsT=wgi[:, gi, kc2, f2 * 128:(f2 + 1) * 128], rhs=xcb[:, kc2, cs:cs + n], start=(kc2 == 0), stop=(kc2 == 1)))(sbk, n, gi, kc2, f2, cs, wgi), reads=[wgk + ("r" if gi == 0 else "i"), pfx + f"xcb{kc2}"], writes=[sk], signal=(kc2 == 1))
                            P.op("act", (lambda sbk, n, cs, tdst, hbias, ft: lambda e: e.activation(out=tdst[:, cs:cs + n], in_=sbk[:, 0:n], func=AF.Tanh, scale=0.5, bias=hbias[:, ft:ft + 1]))(sbk, n, cs, tdst, hbias, ft), reads=[sk, "hbra", "hbri"], writes=[pfx + f"{tkey}{f2}_{cs}"])
                for f2 in range(2):
                    ft = 2 * nb + f2
                    tr = trb[f2]
                    a2 = a2b[f2]
                    trk = [pfx + f"tr{f2}_{cs}" for (cs, n) in chunks_m]
                    P.op("act", (lambda a2, tr, ft: lambda e: e.activation(out=a2, in_=tr, func=AF.Exp, scale=cl[:, ft:ft + 1], bias=cl[:, ft:ft + 1]))(a2, tr, ft), reads=trk + ["cl"], writes=[pfx + f"a2{f2}"])
                    P.op("act", (lambda tr, ft: lambda e: e.activation(out=tr, in_=tr, func=AF.Exp, scale=hcl[:, ft:ft + 1], bias=hcl[:, ft:ft + 1]))(tr, ft), reads=trk + ["hcl", pfx + f"a2{f2}"], writes=[pfx + f"aa{f2}"] + trk)
                    P.op("act", (lambda a2: lambda e: e.activation(out=a2, in_=a2, func=AF.Sqrt, scale=-1.0, bias=one_t[:, :]))(a2), reads=[pfx + f"a2{f2}", "one_t"], writes=[pfx + f"a2{f2}"])
                for f2 in range(2):
                    ft = 2 * nb + f2
                    tr, ti, a2, hh = trb[f2], tib[f2], a2b[f2], hb[f2]
                    trk = [pfx + f"tr{f2}_{cs}" for (cs, n) in chunks_m]
                    tik = [pfx + f"ti{f2}_{cs}" for (cs, n) in chunks_m]
                    if ps == 0:
                        P.op("dve", (lambda a2: lambda e: e.memset(a2[:, 0:1], 1.0))(a2), reads=[], writes=[pfx + f"a2{f2}"])
                    P.op("dve", (lambda ti, f2: lambda e: e.scalar_tensor_tensor(out=ti, in0=ti, scalar=1.0, in1=xc[:, f2, :], op0=ALU.add, op1=ALU.mult))(ti, f2), reads=tik + [pfx + f"xc{f2}"], writes=tik + [pfx + f"uu{f2}"])
                    P.op("dve", (lambda ti, a2: lambda e: e.scalar_tensor_tensor(out=ti, in0=ti, scalar=0.5, in1=a2, op0=ALU.mult, op1=ALU.mult))(ti, a2), reads=[pfx + f"uu{f2}", pfx + f"a2{f2}"], writes=[pfx + f"uu{f2}"] + tik)
                    P.op("dve", (lambda hh, tr, ti, ft: lambda e: e.tensor_tensor_scan(out=hh, data0=tr, data1=ti, initial=hlast[:, ft:ft + 1], op0=ALU.mult, op1=ALU.add))(hh, tr, ti, ft), reads=[pfx + f"aa{f2}", pfx + f"uu{f2}", "hlast"] + trk + tik, writes=[pfx + f"h{f2}"])
                    P.op("dve", (lambda hh, ft: lambda e: e.tensor_copy(out=hlast[:, ft:ft + 1], in_=hh[:, W - 1:W]))(hh, ft), reads=[pfx + f"h{f2}"], writes=["hlast"])

            for f2 in range(2):
                ftp = 14 + f2
                z_tile(load_win(OFF_GR + 128 * ftp), rhs_h, rhs_h_keys, z_chunks(ps, False), gr_epi(ftp))

            preload(load_win(OFF_G))
            preload(load_win(OFF_G + D))
            P.barrier()
            zpool[0] = [(Zb[0], "Z0"), (Zb[1], "Z1"), (Zb[2], "Z2")] + [(OB, "OBz"), (DB, "DBz"), (Sb[0], "S0"), (Sb[1], "S1"), (Sb[2], "S2")]
            RW.off = RW_MARK
            MX = RW.take((16, NT), BF16)
            tA = [RW.take((NT,), F32)] * 2
            tB = [RW.take((NT,), F32)] * 2
            m1 = [RW.take((512,), F32) for _ in range(2)]
            m2 = [RW.take((512,), F32) for _ in range(2)]

            def mg_epi(dst, dkey, ct):
                def epi(zb, zk, cs, n):
                    t0 = cs - OFF
                    P.op("act", lambda e: e.activation(out=dst[:, t0:t0 + n], in_=zb[:, 0:n], func=AF.Tanh, scale=0.5, bias=hbin_c[:, ct:ct + 1]), reads=[zk, "hbin_c"], writes=[dkey + f"_{t0}"])
                return epi

            def load_w(view, c0):
                def f(w, wk):
                    P.dma("pool", wk, lambda e: e.dma_start(out=w[:], in_=view[:, :, c0:c0 + 128]), writes=[wk])
                return f

            def y_epi(tsrc, tkey, mdst, mkey_fn, final_c=None, mother=None):
                def epi(zb, zk, t0, n):
                    i = (t0 // 512) % 2
                    P.op("dve", lambda e: e.scalar_tensor_tensor(out=mdst[i][:, 0:n], in0=tsrc[:, t0:t0 + n], scalar=1.0, in1=zb[:, 0:n], op0=ALU.add, op1=ALU.mult), reads=[zk, tkey + f"_{t0}"], writes=[mkey_fn(i)])
                    if final_c is not None:
                        P.op("dve", lambda e: e.tensor_tensor(out=MX[:, final_c, t0:t0 + n], in0=mother[i][:, 0:n], in1=mdst[i][:, 0:n], op=ALU.add), reads=[pfx + f"m1_{i}", pfx + f"m2_{i}"], writes=[pfx + f"MX{final_c}_{t0}"])
                return epi

            ychunks = [(0, 512), (512, 512)]
            for c in range(16):
                i = c % 2
                z_tile(load_win(OFF_G + 128 * c), rhs_h, rhs_h_keys, z_chunks(ps, False), mg_epi(tA[i], pfx + "tA", OFF_G // 128 + c))
                z_tile(load_win(OFF_G + D + 128 * c), rhs_h, rhs_h_keys, z_chunks(ps, False), mg_epi(tB[i], pfx + "tB", (OFF_G + D) // 128 + c))
                z_tile(load_w(w_rnn_v, 128 * c), lambda kc, cs, n: AT[:, kc, cs:cs + n], lambda kc, cs, n: [pfx + f"AT{kc}_{cs}"], ychunks,
                       y_epi(tA[i], pfx + "tA", m1, lambda ii: pfx + f"m1_{ii}"))
                z_tile(load_w(w_att_v, 128 * c), lambda kc, cs, n: OG[:, kc, cs:cs + n], lambda kc, cs, n: [pfx + f"OG{kc}_{cs}"], ychunks,
                       y_epi(tB[i], pfx + "tB", m2, lambda ii: pfx + f"m2_{ii}", final_c=c, mother=m1))

            P.barrier()
            if DEBUG:
                P.dma("sp", "dbg1", lambda e: e.dma_start(out=dbg["dbg_hT"][ps], in_=hT), writes=["dbg1"])
                P.dma("sp", "dbg2", lambda e: e.dma_start(out=dbg["dbg_AT"][ps], in_=AT), writes=["dbg2"])
                P.dma("sp", "dbg3", lambda e: e.dma_start(out=dbg["dbg_OG"][ps], in_=OG), writes=["dbg3"])
                P.dma("sp", "dbg4", lambda e: e.dma_start(out=dbg["dbg_MX"][ps], in_=MX), writes=["dbg4"])
                P.dma("sp", "dbg5", lambda e: e.dma_start(out=dbg["dbg_KT"][ps], in_=KT), writes=["dbg5"])
                P.dma("sp", "dbg6", lambda e: e.dma_start(out=dbg["dbg_Vt"][ps], in_=Vt), writes=["dbg6"])
                P.barrier()
            RH.reset()
            Gp = RH.take((D,), F32)
            Bp = RH.take((D,), F32)
            LG = RH.take((D,), F32)
            LB = RH.take((D,), F32)
            RAO.reset()
            wo = RAO.take((4, 16, 512), BF16)
            RW.off = RW_MARK
            _mx = RW.take((16, NT), BF16)
            xf0 = RW.take((D,), F32)
            yf0 = RW.take((D,), F32)
            xfb = [xf0, wv[:].rearrange("p a b -> p (a b)").bitcast(F32)]
            yfb = [yf0, wz_all[:, 0:2].rearrange("p a b c -> p (a b c)").bitcast(F32)]
            xf, yf = xf0, yf0
            nmr = sinkst[:, 0:1]
            for cg in range(4):
                P.dma("pool", pfx + f"wo{cg}", (lambda cg: lambda e: e.dma_start(out=wo[:, cg, :, :], in_=w_o_v[:, :, cg * 512:(cg + 1) * 512]))(cg), writes=[pfx + f"wo{cg}"])
            P.dma("sp", dsem(), lambda e: e.dma_start(out=Gp, in_=gebc_d), writes=[pfx + "Gp"])
            P.dma("sp", dsem(), lambda e: e.dma_start(out=Bp, in_=bebc_d), writes=[pfx + "Bp"])
            P.dma("sp", dsem(), lambda e: e.dma_start(out=yf0, in_=bobc_d), writes=[pfx + "yf0"])
            P.dma("sp", dsem(), lambda e: e.dma_start(out=LG, in_=lgbc_d), writes=[pfx + "LG"])
            P.dma("sp", dsem(), lambda e: e.dma_start(out=LB, in_=lbbc_d), writes=[pfx + "LB"])
            P.op("dve", lambda e: e.tensor_scalar(out=Gp, in0=Gp, scalar1=ALPHA, scalar2=None, op0=ALU.mult), reads=[pfx + "Gp"], writes=[pfx + "Gp"])
            P.op("dve", lambda e: e.scalar_tensor_tensor(out=Bp, in0=Bp, scalar=ALPHA, in1=yf0, op0=ALU.mult, op1=ALU.add), reads=[pfx + "Bp", pfx + "yf0"], writes=[pfx + "Bp"])
            banks = [(Zb[0], "Z0"), (Zb[1], "Z1"), (Zb[2], "Z2"), (Sb[0], "S0"), (Sb[1], "S1"), (Sb[2], "S2"), (OB, "OBf"), (DB, "DBf")]
            def load_x(tb):
                r0 = ps * NT + tb * 128
                xb_ = xfb[tb % 2]
                P.dma("sp", dsem(), (lambda r0, xb_: lambda e: e.dma_start(out=xb_, in_=x_d[r0:r0 + 128, :]))(r0, xb_), writes=[pfx + f"xf{tb % 2}"])

            def out_block(tb, xf, yf, xk_, yk_):
                r0 = ps * NT + tb * 128
                if tb == 0:
                    load_x(0)
                if tb + 1 < 8:
                    load_x(tb + 1)
                P.op("act", lambda e: e.activation(out=xf, in_=xf, func=AF.Copy, accum_out=ssum[:, 0:1]), reads=[xk_], writes=[xk_, "ssum0"])
                P.op("act", lambda e: e.activation(out=yf, in_=xf, func=AF.Square, accum_out=ssum[:, 1:2]), reads=[xk_], writes=[yk_, "ssum1"] + [yk_ + f"_{cg}" for cg in range(4)])
                P.op("dve", lambda e: e.tensor_scalar(out=mv[:, 0:1], in0=ssum[:, 0:1], scalar1=1.0 / D, scalar2=None, op0=ALU.mult), reads=["ssum0"], writes=["mv"])
                P.op("dve", lambda e: e.tensor_tensor(out=ssum[:, 2:3], in0=mv[:, 0:1], in1=mv[:, 0:1], op=ALU.mult), reads=["mv"], writes=["ssum2"])
                P.op("dve", lambda e: e.scalar_tensor_tensor(out=mv[:, 1:2], in0=ssum[:, 1:2], scalar=1.0 / D, in1=ssum[:, 2:3], op0=ALU.mult, op1=ALU.subtract), reads=["ssum1", "ssum2", "mv"], writes=["mv"])
                P.op("act", lambda e: e.activation(out=rstd[:, :], in_=mv[:, 1:2], func=AF.Sqrt, bias=eps_t[:, :]), reads=["mv", "eps_t"], writes=["rstd"])
                P.op("dve", lambda e: e.reciprocal(out=rstd[:, :], in_=rstd[:, :]), reads=["rstd"], writes=["rstd"])
                P.op("dve", lambda e: e.scalar_tensor_tensor(out=xf, in0=xf, scalar=mv[:, 0:1], in1=Gp, op0=ALU.subtract, op1=ALU.mult), reads=[xk_, "mv", pfx + "Gp"], writes=[xk_])
                P.op("dve", lambda e: e.scalar_tensor_tensor(out=xf, in0=xf, scalar=rstd[:, 0:1], in1=Bp, op0=ALU.mult, op1=ALU.add), reads=[xk_, "rstd", pfx + "Bp"], writes=[xk_])
                bks = []
                for cg in range(4):
                    zb, zk = banks[(tb * 4 + cg) % 8]
                    bks.append((zb, zk))
                    for kc in range(16):
                        P.op("pe", (lambda zb, kc, tb, cg: lambda e: e.matmul(zb[:, :], lhsT=MX[:, kc, tb * 128:(tb + 1) * 128], rhs=wo[:, cg, kc, :], start=(kc == 0), stop=(kc == 15)))(zb, kc, tb, cg), reads=[pfx + f"wo{cg}", pfx + f"MX{kc}_{(tb // 4) * 512}"], writes=[zk], signal=(kc == 15))
                for cg in range(4):
                    zb, zk = bks[cg]
                    P.op("dve", (lambda zb, cg: lambda e: e.scalar_tensor_tensor(out=yf[:, cg * 512:(cg + 1) * 512], in0=zb[:, :], scalar=0.25, in1=xf[:, cg * 512:(cg + 1) * 512], op0=ALU.mult, op1=ALU.add))(zb, cg), reads=[zk, xk_], writes=[yk_ + f"_{cg}"] + ([yk_] if cg == 0 else []))
                yk = [yk_ + f"_{cg}" for cg in range(4)]
                P.op("act", lambda e: e.activation(out=yf, in_=yf, func=AF.Copy, accum_out=ssum[:, 0:1]), reads=[yk_ + f"_{cg}" for cg in range(4)], writes=[yk_, "ssum0"] + [yk_ + f"_{cg}" for cg in range(4)])
                P.op("act", lambda e: e.activation(out=xf, in_=yf, func=AF.Square, accum_out=ssum[:, 1:2]), reads=[yk_], writes=[xk_, "ssum1"])
                P.op("dve", lambda e: e.tensor_scalar(out=mv[:, 0:1], in0=ssum[:, 0:1], scalar1=1.0 / D, scalar2=None, op0=ALU.mult), reads=["ssum0"], writes=["mv"])
                P.op("dve", lambda e: e.tensor_tensor(out=ssum[:, 2:3], in0=mv[:, 0:1], in1=mv[:, 0:1], op=ALU.mult), reads=["mv"], writes=["ssum2"])
                P.op("dve", lambda e: e.scalar_tensor_tensor(out=mv[:, 1:2], in0=ssum[:, 1:2], scalar=1.0 / D, in1=ssum[:, 2:3], op0=ALU.mult, op1=ALU.subtract), reads=["ssum1", "ssum2", "mv"], writes=["mv"])
                P.op("act", lambda e: e.activation(out=rstd[:, :], in_=mv[:, 1:2], func=AF.Sqrt, bias=eps_t[:, :]), reads=["mv", "eps_t"], writes=["rstd"])
                P.op("dve", lambda e: e.reciprocal(out=rstd[:, :], in_=rstd[:, :]), reads=["rstd"], writes=["rstd"])
                P.op("dve", lambda e: e.scalar_tensor_tensor(out=yf, in0=yf, scalar=mv[:, 0:1], in1=LG, op0=ALU.subtract, op1=ALU.mult), reads=yk + ["mv", pfx + "LG"], writes=[yk_] + yk)
                P.op("dve", lambda e: e.scalar_tensor_tensor(out=yf, in0=yf, scalar=rstd[:, 0:1], in1=LB, op0=ALU.mult, op1=ALU.add), reads=[yk_, "rstd", pfx + "LB"], writes=[yk_] + yk)
                P.dma("sp", dsem(), (lambda r0: lambda e: e.dma_start(out=out_d[r0:r0 + 128, :], in_=yf))(r0), reads=[yk_] + yk, writes=[f"outd{r0}"])
            for tb in range(8):
                out_block(tb, xfb[tb % 2], yfb[tb % 2], pfx + f"xf{tb % 2}", pfx + f"yf{tb % 2}")
        for ps in range(2):
            do_pass(ps)
        P.barrier()
        P.emit()
    return nc


def _host_layout(inp):
    f = np.float32
    x = np.ascontiguousarray(inp["x"], dtype=f)
    cols = lambda v: np.ascontiguousarray(np.asarray(v, f).reshape(-1, 128).T)
    bc = lambda v: np.ascontiguousarray(np.broadcast_to(np.asarray(v, f).reshape(1, -1), (128, np.asarray(v).size)))
    b_in = np.asarray(inp["b_in"], f)[0]
    bk = np.stack([np.tile(b_in[OFF_K + 64 * kh:OFF_K + 64 * kh + 64], 2) for kh in range(4)], axis=1)
    conv_w = np.asarray(inp["conv_w"], f)[0]
    convw = np.ascontiguousarray(conv_w.reshape(4, 16, 128).transpose(2, 1, 0).reshape(128, 64))
    sinks = np.asarray(inp["sinks"], f)[0]
    sink_rows = np.zeros((8, 512), f)
    for kh in range(4):
        for half in range(2):
            for i in range(4):
                sink_rows[kh * 2 + half, i * 128:(i + 1) * 128] = sinks[8 * kh + 2 * i + half]
    ident = np.eye(128, dtype=f)
    rotm = np.zeros((128, 128), f)
    for m in range(128):
        d = m % 64
        if d < 32:
            rotm[m + 32, m] = -1.0
        else:
            rotm[m - 32, m] = 1.0
    s = np.arange(128)[:, None]
    q = np.arange(128)[None, :]
    mask_c = (s <= q).astype(ml_dtypes.bfloat16)
    mask_p = (s > q).astype(ml_dtypes.bfloat16)
    half = 32
    inv = (10000.0 ** (-np.arange(half, dtype=f) / half)).astype(f)
    pos = np.arange(NMETA + SEQ, dtype=f)
    ang = (pos[None, :] * inv[:, None]).astype(f)
    cos_t = np.ascontiguousarray(np.tile(np.cos(ang).astype(f), (4, 1)))
    sin_t = np.ascontiguousarray(np.tile(np.sin(ang).astype(f), (4, 1)))
    common = {
        "meta": np.ascontiguousarray(inp["meta_tokens"], dtype=f),
        "w_in": np.ascontiguousarray(inp["w_in"][0], dtype=f),
        "w_ra": np.ascontiguousarray(inp["w_ra"][0], dtype=f),
        "w_ri": np.ascontiguousarray(inp["w_ri"][0], dtype=f),
        "w_rnn_out": np.ascontiguousarray(inp["w_rnn_out"][0], dtype=f),
        "w_attn_out": np.ascontiguousarray(inp["w_attn_out"][0], dtype=f),
        "w_o": np.ascontiguousarray(inp["w_o"][0], dtype=f),
        "bin_cols": cols(b_in),
        "bk_dup": np.ascontiguousarray(bk),
        "convw": convw,
        "convb": cols(inp["conv_b"][0]),
        "bra": cols(inp["b_ra"][0]),
        "bri": cols(inp["b_ri"][0]),
        "lam": cols(inp["lru_lambda"][0]),
        "ge_cols": cols(inp["ln_emb_g"]),
        "be_cols": cols(inp["ln_emb_b"]),
        "bv_bc": bc(b_in[OFF_V:OFF_V + 256]),
        "sink_rows": sink_rows,
        "ge_bc": bc(inp["ln_emb_g"]),
        "be_bc": bc(inp["ln_emb_b"]),
        "bo_bc": bc(inp["b_o"][0]),
        "lg_bc": bc(inp["ln_g"][0]),
        "lb_bc": bc(inp["ln_b"][0]),
        "ident": ident,
        "rotm": rotm,
        "mask_c": mask_c,
        "mask_p": mask_p,
        "cos_t": cos_t,
        "sin_t": sin_t,
    }
    maps = []
    for b in range(8):
        m = dict(common)
        m["x"] = np.ascontiguousarray(x[b])
        maps.append(m)
    return maps


def kernel(**inputs):
    nc = build_program()
    maps = _host_layout(inputs)
    res = run_bass_kernel_spmd(nc, maps, core_ids=list(range(8)))
    out = np.stack([np.asarray(r["out"], dtype=np.float32) for r in res.results], axis=0)
    return out
```
